# Optimizing a Trainium2 kernel written in Bass

```python
import functools
import jax, jax.numpy as jnp
from jax import lax
import numpy as np

D_MODEL = 1024
BATCH = 1
SEQ = 16384
DEPTH = 1
DEC_BATCH = 128
DEC_SEQ = 8
PAST_LEN = 16384
PAGE_SIZE = 128

N_META = 16
N_HEADS = 16
HEAD_DIM = 64
N_KV_HEADS = 4
GROUP = N_HEADS // N_KV_HEADS
WINDOW = 128
ATTN_WIDTH = N_HEADS * HEAD_DIM
KV_WIDTH = N_KV_HEADS * HEAD_DIM
HG_EXPAND = 128
HG_HEADS = D_MODEL // HG_EXPAND
HG_KDIM = HG_EXPAND
HG_VDIM = D_MODEL // HG_HEADS
HG_FWIDTH = HG_HEADS * HG_KDIM
HG_VWIDTH = HG_HEADS * HG_VDIM
HG_CHUNK = 64
D_FF = 2816
CONV_W = 3
SPLIT_SIZES = (ATTN_WIDTH, KV_WIDTH, KV_WIDTH, HG_FWIDTH, HG_FWIDTH, HG_VWIDTH, HG_VWIDTH, D_MODEL, D_MODEL)
IN_COLS = sum(SPLIT_SIZES)
DEEPNORM_ALPHA = (2.0 * DEPTH) ** 0.25
DEEPNORM_BETA = (8.0 * DEPTH) ** -0.25
LN_EPS = 1e-5
RMS_EPS = 1e-6
NEG_INF = -1e30

kernel_name = 'hybrid_swa_sink_hgrn2_convffn_step'


def layer_norm(x, g, b):
    xf = x.astype(jnp.float32)
    mu = jnp.mean(xf, axis=-1, keepdims=True)
    var = jnp.mean(jnp.square(xf - mu), axis=-1, keepdims=True)
    return ((xf - mu) * lax.rsqrt(var + LN_EPS) * g.astype(jnp.float32) + b.astype(jnp.float32)).astype(x.dtype)


def rms_norm(x, g):
    xf = x.astype(jnp.float32)
    return xf * lax.rsqrt(jnp.mean(xf * xf, axis=-1, keepdims=True) + RMS_EPS) * g.astype(jnp.float32)


def split_columns(z):
    offsets = np.cumsum(SPLIT_SIZES)[:-1].tolist()
    return jnp.split(z, offsets, axis=-1)


def sink_attention(q, k, v, mask, sinks):
    s = jnp.einsum('...tkgd,...skd->...kgts', q, k, preferred_element_type=jnp.float32) * (HEAD_DIM ** -0.5)
    s = jnp.where(mask, s, NEG_INF)
    sink = jnp.broadcast_to(sinks.astype(jnp.float32).reshape(N_KV_HEADS, GROUP, 1, 1), s.shape[:-1] + (1,))
    p = jax.nn.softmax(jnp.concatenate([s, sink], axis=-1), axis=-1)[..., :-1]
    return jnp.einsum('...kgts,...skd->...tkgd', p.astype(v.dtype), v)


def swa_prompt(q, k, v, sinks):
    B, L = q.shape[:2]
    pad = (-N_META) % WINDOW
    nb = (L + pad) // WINDOW
    padt = lambda a: jnp.pad(a, ((0, 0), (pad, 0), (0, 0), (0, 0)))
    qb = padt(q).reshape(B, nb, WINDOW, N_KV_HEADS, GROUP, HEAD_DIM)

    def band_keys(a):
        blocks = padt(a).reshape(B, nb, WINDOW, N_KV_HEADS, HEAD_DIM)
        prev = jnp.pad(blocks, ((0, 0), (1, 0), (0, 0), (0, 0), (0, 0)))[:, :-1]
        meta = jnp.broadcast_to(a[:, None, :N_META], (B, nb, N_META, N_KV_HEADS, HEAD_DIM))
        return jnp.concatenate([meta, prev, blocks], axis=2)

    kk = band_keys(k)
    vv = band_keys(v)
    qpos = (jnp.arange(nb * WINDOW) - pad).reshape(nb, WINDOW)
    qp = qpos[:, :, None]
    kpos = jnp.concatenate([qpos - WINDOW, qpos], axis=1)[:, None, :]
    band_ok = (kpos >= N_META) & (kpos <= qp) & (kpos > qp - WINDOW)
    meta_ok = jnp.arange(N_META)[None, None, :] <= qp
    mask = jnp.concatenate([meta_ok, band_ok], axis=-1)
    o = sink_attention(qb, kk, vv, mask[None, :, None, None], sinks)
    o = o.reshape(B, nb * WINDOW, ATTN_WIDTH)[:, pad:]
    return o, (k[:, :N_META], v[:, :N_META], k[:, -WINDOW:], v[:, -WINDOW:])


def swa_sample(q, k, v, meta_k, meta_v, win_k, win_v, sinks):
    B, T = q.shape[:2]
    band_k = jnp.concatenate([win_k.astype(k.dtype), k], axis=1)
    band_v = jnp.concatenate([win_v.astype(v.dtype), v], axis=1)
    kk = jnp.concatenate([meta_k.astype(k.dtype), band_k], axis=1)
    vv = jnp.concatenate([meta_v.astype(v.dtype), band_v], axis=1)
    j = jnp.arange(T)[:, None]
    idx = jnp.arange(WINDOW + T)[None, :]
    band_ok = (idx > j) & (idx <= WINDOW + j) & (PAST_LEN - WINDOW + idx >= N_META)
    mask = jnp.concatenate([jnp.ones((T, N_META), bool), band_ok], axis=1)
    o = sink_attention(q.reshape(B, T, N_KV_HEADS, GROUP, HEAD_DIM), kk, vv, mask, sinks)
    return o.reshape(B, T, ATTN_WIDTH), (band_k[:, -WINDOW:], band_v[:, -WINDOW:])


def hgrn_gates(hq, hf, hi, lb):
    B, T = hq.shape[:2]
    f = lb + (1.0 - lb) * jax.nn.sigmoid(hf.astype(jnp.float32))
    heads = lambda a, d: a.reshape(B, T, HG_HEADS, d)
    q = heads(jax.nn.silu(hq.astype(jnp.float32)), HG_KDIM)
    k = heads(1.0 - f, HG_KDIM)
    log_f = heads(jnp.log(f), HG_KDIM)
    v = heads(hi.astype(jnp.float32), HG_VDIM)
    return q, k, v, log_f


def hgrn_prompt(q, k, v, log_f):
    B, L = q.shape[:2]
    pad = (-N_META) % HG_CHUNK
    padt = lambda a: jnp.pad(a, ((0, 0), (pad, 0), (0, 0), (0, 0)))
    nc = (L + pad) // HG_CHUNK
    rs = lambda a: padt(a).reshape(B, nc, HG_CHUNK, HG_HEADS, a.shape[-1])
    q, k, v, log_f = rs(q), rs(k), rs(v), rs(log_f)
    b = jnp.cumsum(log_f, axis=2)
    b_last = b[:, :, -1:]
    q_d = q * jnp.exp(b)
    k_d = k * jnp.exp(-b)
    k_end = k * jnp.exp(b_last - b)
    causal = jnp.tril(jnp.ones((HG_CHUNK, HG_CHUNK), bool))
    scores = jnp.where(causal, jnp.einsum('bnthd,bnshd->bnhts', q_d, k_d), 0.0)
    o_intra = jnp.einsum('bnhts,bnshv->bnthv', scores, v)
    chunk_state = jnp.einsum('bnshd,bnshv->bnhdv', k_end, v)
    decay = jnp.exp(b_last[:, :, 0])

    def step(S, inp):
        dec, cs = inp
        return dec[..., None] * S + cs, S

    S0 = jnp.zeros((B, HG_HEADS, HG_KDIM, HG_VDIM), jnp.float32)
    S_final, S_prev = lax.scan(step, S0, (jnp.swapaxes(decay, 0, 1), jnp.swapaxes(chunk_state, 0, 1)))
    o_inter = jnp.einsum('bnthd,bnhdv->bnthv', q_d, jnp.swapaxes(S_prev, 0, 1))
    o = (o_intra + o_inter).reshape(B, nc * HG_CHUNK, HG_HEADS, HG_VDIM)[:, pad:]
    return o, S_final


def hgrn_sample(q, k, v, log_f, s0):
    def step(S, inp):
        q_t, k_t, v_t, lf_t = inp
        S = jnp.exp(lf_t)[..., None] * S + k_t[..., None] * v_t[..., None, :]
        return S, jnp.einsum('bhd,bhdv->bhv', q_t, S)

    tm = lambda a: jnp.swapaxes(a, 0, 1)
    S, o = lax.scan(step, s0.astype(jnp.float32), (tm(q), tm(k), tm(v), tm(log_f)))
    return tm(o), S


def conv_ffn(x, conv_buf, w_up_l, conv_w_l, conv_b_l, w_down_l):
    T = x.shape[1]
    u, val = jnp.split(x @ w_up_l, 2, axis=-1)
    u_ext = jnp.concatenate([conv_buf.astype(u.dtype), u], axis=1)
    c = conv_b_l + sum(conv_w_l[j] * u_ext[:, j:j + T] for j in range(CONV_W))
    h = jax.nn.gelu(c) * val
    return h @ w_down_l, u_ext[:, -(CONV_W - 1):]


def trunk_layer(x, attn_fn, hgrn_fn, conv_buf, lb, w_in_l, norm_g_l, w_out_l, ln1_g_l, ln1_b_l,
                w_up_l, conv_w_l, conv_b_l, w_down_l, ln2_g_l, ln2_b_l):
    B, T = x.shape[:2]
    qa, ka, va, hq, hf, hi, hg, ga, gb = split_columns(x @ w_in_l)
    o_attn, attn_state = attn_fn(qa.reshape(B, T, N_HEADS, HEAD_DIM),
                                 ka.reshape(B, T, N_KV_HEADS, HEAD_DIM),
                                 va.reshape(B, T, N_KV_HEADS, HEAD_DIM))
    q, k, v, log_f = hgrn_gates(hq, hf, hi, lb)
    o_hg, hg_state = hgrn_fn(q, k, v, log_f)
    o_hg = rms_norm(o_hg.reshape(B, T, HG_VWIDTH), norm_g_l).astype(x.dtype) * jax.nn.silu(hg)
    merged = jax.nn.sigmoid(ga) * o_attn + jax.nn.sigmoid(gb) * o_hg
    x = layer_norm(DEEPNORM_ALPHA * x + merged @ w_out_l, ln1_g_l, ln1_b_l)
    ffn_out, conv_state = conv_ffn(x, conv_buf, w_up_l, conv_w_l, conv_b_l, w_down_l)
    x = layer_norm(DEEPNORM_ALPHA * x + ffn_out, ln2_g_l, ln2_b_l)
    return x, attn_state, hg_state, conv_state


def setup_inputs(seed: int = 0) -> dict:
    key = jax.random.key(seed)
    ks = jax.random.split(key, 22)

    def nrm(k, shape, scale):
        return jax.random.normal(k, shape, jnp.float32) * scale

    meta_shape = (DEPTH, DEC_BATCH, N_META, N_KV_HEADS, HEAD_DIM)
    win_shape = (DEPTH, DEC_BATCH, WINDOW, N_KV_HEADS, HEAD_DIM)
    return {
        'x_prompt': nrm(ks[0], (BATCH, SEQ, D_MODEL), 1.0),
        'x_sample': nrm(ks[1], (DEC_BATCH, DEC_SEQ, D_MODEL), 1.0),
        'cache_meta_k': nrm(ks[2], meta_shape, 1.0),
        'cache_meta_v': nrm(ks[3], meta_shape, 1.0),
        'cache_win_k': nrm(ks[4], win_shape, 1.0),
        'cache_win_v': nrm(ks[5], win_shape, 1.0),
        'state_hgrn': nrm(ks[6], (DEPTH, DEC_BATCH, HG_HEADS, HG_KDIM, HG_VDIM), 0.5),
        'state_conv': nrm(ks[7], (DEPTH, DEC_BATCH, CONV_W - 1, D_FF), 1.0),
        'meta_tokens': nrm(ks[8], (N_META, D_MODEL), 1.0),
        'w_in': nrm(ks[9], (DEPTH, D_MODEL, IN_COLS), D_MODEL ** -0.5),
        'attn_sinks': nrm(ks[10], (DEPTH, N_HEADS), 0.5),
        'hgrn_lb': nrm(ks[11], (DEPTH + 1, HG_FWIDTH), 0.1),
        'hgrn_norm_g': 1.0 + nrm(ks[12], (DEPTH, HG_VWIDTH), 0.05),
        'w_out': nrm(ks[13], (DEPTH, D_MODEL, D_MODEL), DEEPNORM_BETA * D_MODEL ** -0.5),
        'ln1_g': 1.0 + nrm(ks[14], (DEPTH, D_MODEL), 0.05),
        'ln1_b': nrm(ks[15], (DEPTH, D_MODEL), 0.02),
        'w_up': nrm(ks[16], (DEPTH, D_MODEL, 2 * D_FF), D_MODEL ** -0.5),
        'conv_w': nrm(ks[17], (DEPTH, CONV_W, D_FF), CONV_W ** -0.5),
        'conv_b': nrm(ks[18], (DEPTH, D_FF), 0.02),
        'w_down': nrm(ks[19], (DEPTH, D_FF, D_MODEL), DEEPNORM_BETA * D_FF ** -0.5),
        'ln2_g': 1.0 + nrm(ks[20], (DEPTH, D_MODEL), 0.05),
        'ln2_b': nrm(ks[21], (DEPTH, D_MODEL), 0.02),
    }


def reference(x_prompt, x_sample, cache_meta_k, cache_meta_v, cache_win_k, cache_win_v, state_hgrn, state_conv,
              meta_tokens, w_in, attn_sinks, hgrn_lb, hgrn_norm_g, w_out, ln1_g, ln1_b,
              w_up, conv_w, conv_b, w_down, ln2_g, ln2_b):
    lb_all = jnp.cumsum(jax.nn.softmax(hgrn_lb.astype(jnp.float32), axis=0), axis=0)
    B = x_prompt.shape[0]
    xp = jnp.concatenate([jnp.broadcast_to(meta_tokens.astype(x_prompt.dtype), (B, N_META, D_MODEL)), x_prompt], axis=1)
    xs = x_sample
    p_states = []
    s_states = []
    for l in range(DEPTH):
        shared = (lb_all[l], w_in[l], hgrn_norm_g[l], w_out[l], ln1_g[l], ln1_b[l],
                  w_up[l], conv_w[l], conv_b[l], w_down[l], ln2_g[l], ln2_b[l])
        xp, p_attn, p_hg, p_conv = trunk_layer(
            xp, functools.partial(swa_prompt, sinks=attn_sinks[l]), hgrn_prompt,
            jnp.zeros((B, CONV_W - 1, D_FF), xp.dtype), *shared)
        xs, s_attn, s_hg, s_conv = trunk_layer(
            xs,
            functools.partial(swa_sample, meta_k=cache_meta_k[l], meta_v=cache_meta_v[l],
                              win_k=cache_win_k[l], win_v=cache_win_v[l], sinks=attn_sinks[l]),
            functools.partial(hgrn_sample, s0=state_hgrn[l]),
            state_conv[l], *shared)
        p_states.append(p_attn + (p_hg, p_conv))
        s_states.append(s_attn + (s_hg, s_conv))
    meta_k_prompt = jnp.stack([s[0] for s in p_states])
    meta_v_prompt = jnp.stack([s[1] for s in p_states])
    win_k_prompt = jnp.stack([s[2] for s in p_states])
    win_v_prompt = jnp.stack([s[3] for s in p_states])
    hgrn_prompt_state = jnp.stack([s[4] for s in p_states])
    conv_prompt_state = jnp.stack([s[5] for s in p_states])
    win_k_sample = jnp.stack([s[0] for s in s_states])
    win_v_sample = jnp.stack([s[1] for s in s_states])
    hgrn_sample_state = jnp.stack([s[2] for s in s_states])
    conv_sample_state = jnp.stack([s[3] for s in s_states])
    y_prompt = xp[:, N_META:]
    y_sample = xs
    return (y_prompt, y_sample, meta_k_prompt, meta_v_prompt, win_k_prompt, win_v_prompt,
            hgrn_prompt_state, conv_prompt_state, win_k_sample, win_v_sample, hgrn_sample_state, conv_sample_state)
```

```python
import contextlib
import os
import numpy as np
import concourse.bass as bass
import concourse.mybir as mybir
from concourse.bass_utils import run_bass_kernel_spmd

F32 = mybir.dt.float32
BF16 = mybir.dt.bfloat16
AF = mybir.ActivationFunctionType
ALU = mybir.AluOpType
AX = mybir.AxisListType

NCORES = 8
D = 1024
DFF = 2816
NFF = 22
ALPHA = 2.0 ** 0.25
LN_EPS = 1e-5
RMS_EPS = 1e-6
NU_W = 41


class Trk:
    def __init__(self, nc, es):
        self.nc = nc
        self.es = es
        self.engs = {"pe": nc.tensor, "act": nc.scalar, "dve": nc.vector, "pool": nc.gpsimd, "sp": nc.sync}
        self.sem = {}
        self.cnt = {}
        for k in self.engs:
            self.sem[k] = es.enter_context(nc.semaphore("s_" + k))
            self.cnt[k] = 0
        self.waited = {k: {} for k in self.engs}
        self.lastw = {}
        self.readers = {}

    def _need(self, reads, writes):
        need = {}

        def add(k, c):
            if c > need.get(k, 0):
                need[k] = c
        for r in reads:
            lw = self.lastw.get(r)
            if lw:
                add(*lw)
        for w in writes:
            lw = self.lastw.get(w)
            if lw:
                add(*lw)
            for k, c in self.readers.get(w, {}).items():
                add(k, c)
        return need

    def _waits(self, e, need, skip_same=False):
        for k, c in need.items():
            if skip_same and k == e:
                continue
            if self.waited[e].get(k, 0) >= c:
                continue
            self.engs[e].wait_ge(self.sem[k], c)
            self.waited[e][k] = c

    def _record(self, key, c, reads, writes):
        for r in reads:
            d = self.readers.setdefault(r, {})
            if c > d.get(key, 0):
                d[key] = c
        for w in writes:
            self.lastw[w] = (key, c)
            self.readers[w] = {}

    def op(self, e, fn, reads=(), writes=(), inc=True):
        need = self._need(reads, writes)
        self._waits(e, need, skip_same=(e == "pe"))
        ins = fn(self.engs[e])
        if inc:
            self.cnt[e] += 1
            ins.then_inc(self.sem[e], 1)
            self._record(e, self.cnt[e], reads, writes)
        else:
            self._record(e, self.cnt[e] + 1, reads, writes)
        return ins

    def dma(self, q, semname, out, in_, reads=(), writes=(), **kw):
        key = "dma_" + semname
        if key not in self.sem:
            self.sem[key] = self.es.enter_context(self.nc.semaphore("sd_" + semname))
            self.cnt[key] = 0
        need = self._need(reads, writes)
        self._waits(q, need)
        ins = self.engs[q].dma_start(out=out, in_=in_, **kw)
        self.cnt[key] += 16
        ins.then_inc(self.sem[key], 16)
        self._record(key, self.cnt[key], reads, writes)
        return ins

    def prewait(self, e, writes=(), reads=()):
        self._waits(e, self._need(reads, writes), skip_same=(e == "pe"))

    def fence(self, src, dst):
        for dk in dst:
            d = self.readers.setdefault(dk, {})
            for sk in src:
                lw = self.lastw.get(sk)
                if lw and lw[1] > d.get(lw[0], 0):
                    d[lw[0]] = lw[1]
                for k, c in self.readers.get(sk, {}).items():
                    if c > d.get(k, 0):
                        d[k] = c

    def barrier(self):
        for e in self.engs:
            need = {k: c for k, c in self.cnt.items() if c > 0 and k != e}
            self._waits(e, need)

    def finish(self, e="sp"):
        for k in self.sem:
            if k.startswith("dma_") and self.cnt[k] > 0:
                if self.waited[e].get(k, 0) < self.cnt[k]:
                    self.engs[e].wait_ge(self.sem[k], self.cnt[k])


def _win_chunks():
    ch = []
    for h in range(8):
        ch += [("hq", h), ("hf", h), ("q", h)]
    ch += [("kd", i) for i in range(4)]
    ch += [("hg", h) for h in range(8)] + [("ga", i) for i in range(8)] + [("gb", h) for h in range(8)]
    return ch


WIN_CHUNKS = _win_chunks()
UP_CHUNKS = [("u", 0), ("u", 1)] + [c for m in range(2, NFF) for c in (("val", m - 2), ("u", m))] + \
            [("val", NFF - 2), ("val", NFF - 1)]
U_F0, U_KV, U_HI0, U_HI1, U_O0, U_O1, U_UP0, U_DN0, U_UPW0 = 0, 13, 14, 15, 16, 17, 18, 29, 35


def build_weight_units(w_in, w_out, w_up, w_down):
    W = np.zeros((NU_W, 128, 8, 512), np.float32)
    w_in = w_in[0]
    wq = w_in[:, 0:1024]; wk = w_in[:, 1024:1280]; wv = w_in[:, 1280:1536]
    whq = w_in[:, 1536:2560]; whf = w_in[:, 2560:3584]; whi = w_in[:, 3584:4608]
    whg = w_in[:, 4608:5632]; wga = w_in[:, 5632:6656]; wgb = w_in[:, 6656:7680]

    def kp(mat):
        return mat.reshape(8, 128, mat.shape[1]).transpose(1, 0, 2)
    src = {"q": wq, "hq": whq, "hf": whf, "hg": whg, "ga": wga, "gb": wgb}
    for ci, (kind, i) in enumerate(WIN_CHUNKS):
        if kind == "kd":
            blk = np.concatenate([wk[:, 64 * i:64 * i + 64]] * 2, axis=1)
        else:
            blk = src[kind][:, 128 * i:128 * i + 128]
        W[U_F0 + ci // 4, :, :, 128 * (ci % 4):128 * (ci % 4) + 128] = kp(blk)
    W[U_KV] = kp(np.concatenate([wk, wv], axis=1))
    W[U_HI0] = kp(whi[:, 0:512])
    W[U_HI1] = kp(whi[:, 512:1024])
    W[U_O0] = kp(w_out[0][:, 0:512])
    W[U_O1] = kp(w_out[0][:, 512:1024])
    wu = w_up[0]
    for ci, (kind, m) in enumerate(UP_CHUNKS):
        off = 0 if kind == "u" else DFF
        W[U_UP0 + ci // 4, :, :, 128 * (ci % 4):128 * (ci % 4) + 128] = kp(wu[:, off + 128 * m:off + 128 * m + 128])
    for m in range(NFF):
        W[U_UPW0 + m // 4, :, :, 128 * (m % 4):128 * (m % 4) + 128] = kp(wu[:, 128 * m:128 * m + 128])
    wd = w_down[0].reshape(NFF, 128, 1024).transpose(1, 0, 2)
    for chh in range(2):
        for kg in range(3):
            nk = 8 if kg < 2 else 6
            W[U_DN0 + chh * 3 + kg, :, 0:nk, :] = wd[:, kg * 8:kg * 8 + nk, chh * 512:(chh + 1) * 512]
    return W.reshape(NU_W, 128, 4096)


def build_program(NSTM=4, with_sample=True, dbg=None):
    NST = NSTM + 1
    NTOK = 512 * NST
    nc = bass.Bass("TRN2", target_bir_lowering=False)

    def din(name, shape, dt=F32):
        return nc.dram_tensor(name, shape, dt, kind="ExternalInput").ap()

    def dout(name, shape, dt=F32):
        return nc.dram_tensor(name, shape, dt, kind="ExternalOutput").ap()

    xT_d = din("xT", [128, 8, NTOK])
    xtok_d = din("xtok", [NTOK, D])
    xmT_d = din("xmT", [128, 8, 16])
    W_d = din("W", [NU_W, 128, 4096])
    masks_d = din("masks", [8, 128, 128])
    rmask_d = din("rmask", [128, 512])
    lbraw_d = din("lbraw", [128, 2, 8])
    normg_d = din("normg", [128, 8])
    sinks_d = din("sinks", [1, 16])
    ln_d = din("lnp", [4, D])
    cw_d = din("cw", [128, NFF, 3])
    cb_d = din("cb", [128, NFF])

    y_d = dout("y", [NSTM * 512, D])
    metakv_d = dout("metakv", [16, 512])
    lastkv_d = dout("lastkv", [128, 512])
    S_d = dout("Sfin", [128, 8, 128])
    conv_d = dout("convp", [128, NFF, 2])

    if with_sample:
        xsT_d = din("xsT", [128, 8, 128])
        xstok_d = din("xstok", [128, D])
        ckT_d = din("ckT", [128, 16, 4, 144])
        cmv_d = din("cmv", [16, 16, 256])
        cwv_d = din("cwv", [128, 16, 256])
        cwk_raw_d = din("cwk_raw", [16, 128 * 256])
        cwv_raw_d = din("cwv_raw", [16, 128 * 256])
        s0_d = din("s0", [128, 16, 8, 128])
        scT_d = din("scT", [128, NFF, 16, 2])
        rmask8_d = din("rmask8", [128, 128])
        rowmask_d = din("rowmask", [128, 16])
        ys_d = dout("ys", [128, D])
        wks_d = dout("wks", [16, 128 * 256])
        wvs_d = dout("wvs", [16, 128 * 256])
        hs_d = dout("hs", [16, 8, 128, 128])
        convs_d = dout("convs", [128, NFF, 16, 2])

    with contextlib.ExitStack() as es:
        T = Trk(nc, es)

        def sb(name, shape, dt):
            return es.enter_context(nc.sbuf_tensor("sb_" + name, shape, dt))

        NB = 8
        pb = [es.enter_context(nc.psum_tensor(f"pb{i}", [128, 512], F32)) for i in range(NB)]
        ptv = [pb[i][:, :].bitcast(BF16) for i in range(NB)]
        bank_ctr = [0]
        reserved = set()

        def nb():
            while True:
                b = bank_ctr[0] % NB
                bank_ctr[0] += 1
                if b not in reserved:
                    return b

        def peek_banks(n):
            out, c = [], bank_ctr[0]
            while len(out) < n:
                if (c % NB) not in reserved:
                    out.append(c % NB)
                c += 1
            return out
        pt_ctr = [0]

        def npt():
            h = pt_ctr[0] % 2
            pt_ctr[0] += 1
            return h

        ident = sb("ident", [128, 128], BF16)
        ones_bf = sb("ones_bf", [128, 128], BF16)
        masks = sb("masks", [128, 8, 128], BF16)
        rmask = sb("rmask", [128, 512], BF16)
        lbraw = sb("lbraw", [128, 2, 8], F32)
        lb = sb("lb", [128, 8], F32)
        oml = sb("oml", [128, 8], F32)
        noml = sb("noml", [128, 8], F32)
        normg = sb("normg", [128, 8], F32)
        sinks = sb("sinks", [1, 16], F32)
        sinkrow = sb("sinkrow", [1, 4, 512], BF16)
        lnp = sb("lnp", [128, 4, D], F32)
        cw = sb("cw", [128, NFF, 3], F32)
        cb = sb("cb", [128, NFF], F32)
        epsln = sb("epsln", [128, 1], F32)
        epsrms = sb("epsrms", [128, 1], F32)
        mhalf = sb("mhalf", [128, 1], F32)

        T.op("dve", lambda e: e.memset(ones_bf[:], 1.0), writes=["ones"])
        T.op("dve", lambda e: e.memset(epsln[:], LN_EPS), writes=["eps"])
        T.op("dve", lambda e: e.memset(epsrms[:], RMS_EPS), writes=["eps"])
        T.op("dve", lambda e: e.memset(mhalf[:], -0.5), writes=["eps"])
        T.op("pool", lambda e: e.memset(ident[:], 1.0), writes=["ident"])
        T.op("pool", lambda e: e.affine_select(out=ident[:], in_=ident[:], pattern=[[-1, 128]],
                                               compare_op=ALU.is_equal, fill=0.0, base=0, channel_multiplier=1),
             reads=["ident"], writes=["ident"])
        T.dma("sp", "c3", lbraw[:], lbraw_d[:, :, :], writes=["lbraw"])
        T.dma("sp", "c4", normg[:], normg_d[:, :], writes=["normg"])
        T.dma("sp", "c5", sinks[:], sinks_d[:, :], writes=["sinks"])
        for i in range(4):
            T.dma("sp", "c6", lnp[:, i, :], ln_d[i:i + 1, :].partition_broadcast(128).rearrange("p a d -> p (a d)"),
                  writes=["lnp"])
        T.dma("sp", "c7", cw[:], cw_d[:, :, :], writes=["cw"])
        T.dma("sp", "c8", cb[:], cb_d[:, :], writes=["cb"])
        T.op("dve", lambda e: e.tensor_tensor(out=lb[:], in0=lbraw[:, 0, :], in1=lbraw[:, 1, :], op=ALU.subtract),
             reads=["lbraw"], writes=["lb"])
        T.op("act", lambda e: e.activation(out=lb[:], in_=lb[:], func=AF.Sigmoid), reads=["lb"], writes=["lb"])
        T.op("dve", lambda e: e.tensor_scalar(out=oml[:], in0=lb[:], scalar1=-1.0, scalar2=1.0, op0=ALU.mult, op1=ALU.add),
             reads=["lb"], writes=["oml"])
        T.op("dve", lambda e: e.tensor_scalar(out=noml[:], in0=lb[:], scalar1=1.0, scalar2=-1.0, op0=ALU.mult, op1=ALU.add),
             reads=["lb"], writes=["oml"])
        T.op("act", lambda e: e.activation(out=sinks[:], in_=sinks[:], func=AF.Exp), reads=["sinks"], writes=["sinks"])
        srv = sinkrow[:].rearrange("p k (a b q) -> p k a b q", a=2, b=2)
        for k in range(4):
            for h2 in range(2):
                for c2 in range(2):
                    hd = 4 * k + 2 * c2 + h2
                    T.op("dve", lambda e: e.tensor_copy(out=srv[0:1, k, h2, c2, :],
                                                        in_=sinks[0:1, hd:hd + 1].broadcast_to([1, 128])),
                         reads=["sinks"], writes=["sinkrow"])

        NWB = 3
        wbuf = [sb(f"wbuf{i}", [128, 8, 512], BF16) for i in range(NWB)]
        sched = []
        for st in range(NST):
            units = list(range(0, 7)) + [U_KV, U_HI0, U_HI1] + list(range(7, 13)) + [U_O0, U_O1]
            if st == 0:
                units += list(range(U_UPW0, U_UPW0 + 6))
            else:
                units += list(range(U_UP0, U_UP0 + 11)) + list(range(U_DN0, U_DN0 + 6))
            sched += units
        if with_sample:
            sched += list(range(0, 7)) + [U_KV, U_HI0, U_HI1] + list(range(7, 13)) + [U_O0, U_O1] + \
                list(range(U_UP0, U_UP0 + 11)) + list(range(U_DN0, U_DN0 + 6))
        ws_state = {"issued": 0, "pos": 0}
        Wb_d = nc.dram_tensor("Wb16", [U_UPW0, 128, 4096], BF16, kind="Internal").ap()
        converted = set()
        pending_store = {}

        def ws_issue_upto(n):
            while ws_state["issued"] < min(n, len(sched)):
                i = ws_state["issued"]
                s = i % NWB
                u = sched[i]
                if u in converted:
                    T.dma("pool", f"w{s}", wbuf[s][:].rearrange("p k c -> p (k c)"), Wb_d[u], reads=[("Wb", u)], writes=[("w", s)])
                else:
                    T.dma("pool", f"w{s}", wbuf[s][:].rearrange("p k c -> p (k c)"), W_d[u], writes=[("w", s)])
                    if u < U_UPW0:
                        converted.add(u)
                        pending_store[i] = u
                ws_state["issued"] += 1

        def ws_next(expect, hold=0):
            i = ws_state["pos"]
            assert sched[i] == expect, (i, sched[i], expect)
            ws_issue_upto(i + 1)
            if i in pending_store:
                u = pending_store.pop(i)
                T.dma("pool", f"wst{i % NWB}", Wb_d[u], wbuf[i % NWB][:].rearrange("p k c -> p (k c)"),
                      reads=[("w", i % NWB)], writes=[("Wb", u)])
            ws_issue_upto(i + NWB - hold)
            ws_state["pos"] += 1
            return i % NWB

        stat6 = sb("stat6", [128, 2, 6], F32)
        mv = sb("mv", [128, 2], F32)
        rs1 = sb("rs1", [128, 1], F32)
        dec = sb("dec", [128, 8, 16], F32)
        uhalo = sb("uhalo", [128, NFF, 2], F32)
        rr = {}

        def rot(name, n):
            v = rr.get(name, 0) % n
            rr[name] = rr.get(name, 0) + 1
            return v

        def layer_norm(src_ap, src_key, gi, dst_ap, dst_key):
            for hh in range(2):
                T.op("dve", lambda e: e.bn_stats(out=stat6[:, hh, :], in_=src_ap[:, hh * 512:(hh + 1) * 512]),
                     reads=[src_key], writes=["stat6"])
            T.op("dve", lambda e: e.bn_aggr(out=mv[:], in_=stat6[:].rearrange("p a s -> p (a s)")),
                 reads=["stat6"], writes=["mv"])
            T.op("dve", lambda e: e.tensor_scalar(out=rs1[:], in0=mv[:, 1:2], scalar1=LN_EPS, scalar2=None, op0=ALU.add),
                 reads=["mv"], writes=["rs1"])
            T.op("pool", lambda e: e.tensor_tensor(out=rs1[:], in0=rs1[:], in1=mhalf[:], op=ALU.pow), reads=["rs1", "eps"], writes=["rs1"])
            T.op("dve", lambda e: e.tensor_scalar(out=src_ap, in0=src_ap, scalar1=mv[:, 0:1], scalar2=rs1[:],
                                                  op0=ALU.subtract, op1=ALU.mult),
                 reads=[src_key, "mv", "rs1"], writes=[src_key])
            T.op("dve", lambda e: e.tensor_tensor(out=src_ap, in0=src_ap, in1=lnp[:, gi, :], op=ALU.mult),
                 reads=[src_key, "lnp"], writes=[src_key])
            T.op("dve", lambda e: e.tensor_tensor(out=dst_ap, in0=src_ap, in1=lnp[:, gi + 1, :], op=ALU.add),
                 reads=[src_key, "lnp"], writes=[dst_key])

        def fm_chunk(wslot, ci4, rhs_of_k, rkeys, N):
            bank = nb()
            for k in range(8):
                T.op("pe", lambda e: e.matmul(pb[bank][:, 0:N], lhsT=wbuf[wslot][:, k, ci4 * 128:(ci4 + 1) * 128],
                                              rhs=rhs_of_k(k), start=(k == 0), stop=(k == 7)),
                     reads=[("w", wslot)] + rkeys, writes=[("pb", bank)], inc=(k == 7))
            return bank

        def tm_group(wslot, lhsT_of_k, lkeys, ncols, M=128):
            bank = nb()
            for k in range(8):
                T.op("pe", lambda e: e.matmul(pb[bank][0:M, 0:ncols], lhsT=lhsT_of_k(k), rhs=wbuf[wslot][:, k, 0:ncols],
                                              start=(k == 0), stop=(k == 7)),
                     reads=[("w", wslot)] + lkeys, writes=[("pb", bank)], inc=(k == 7))
            return bank

        def phase(mode):
            PR = (mode == "prompt")
            N = 512 if PR else 128
            NTL = N // 128
            CL = 64 if PR else 8
            NCK = N // CL
            with contextlib.ExitStack() as ex:
                def sx(name, shape, dt):
                    return ex.enter_context(nc.sbuf_tensor(("sp_" if PR else "ss_") + name, shape, dt))

                xTb = [sx(f"xTb{i}", [128, 8, N], BF16) for i in range(2 if PR else 1)]
                xtok = [sx(f"xtok{i}", [128, D], F32) for i in range(2)]
                bufA = sx("bufA", [128, 24 * N], BF16)
                qT = bufA[:, 0:8 * N].rearrange("p (k t) -> p k t", t=N)
                q_dT = bufA[:, 8 * N:16 * N].rearrange("p (k t) -> p k t", t=N)
                k_dT = bufA[:, 16 * N:24 * N].rearrange("p (k t) -> p k t", t=N)
                hT = bufA[:, 0:NFF * N].rearrange("p (k t) -> p k t", t=N)
                A_KEYS = ["qT"] + [("q_dT", h) for h in range(8)] + [("k_dT", h) for h in range(8)]
                H_KEYS = [("hT", m) for m in range(NFF)]
                hv = sx("hv", [128, NTL, 1024], BF16)
                bufC = sx("bufC", [128, 8 * N], BF16)
                ke = bufC[:].rearrange("p (j h d) -> p j h d", j=NTL, h=8)
                x1T = bufC[:].rearrange("p (k t) -> p k t", t=N)
                KE_KEYS = [("ke", j) for j in range(NTL)]
                X1T_KEYS = ["x1T"]
                oaT = sx("oaT", [128, 8, N], BF16)
                arB = sx("arB", [128, 8 * N], F32)
                ohT = arB[:].rearrange("p (h t) -> p h t", t=N)
                x1 = arB[:].rearrange("p (j d) -> p j d", d=D)
                OH_KEYS = [("ohT", h) for h in range(8)]
                X1_KEYS = [("x1", j) for j in range(NTL)]
                NKV = 1 if PR else 2
                kv32 = [sx(f"kv32_{i}", [128, 512], F32) for i in range(NKV)]
                qh = [sx(f"qh{i}", [128, N], F32) for i in range(2)]
                sg = [sx(f"sg{i}", [128, N], F32) for i in range(2)]
                logf = sx("logf", [128, N], F32)
                kk = sx("kk", [128, N], F32)
                bb = sx("bb", [128, 512], F32)
                eb = sx("eb", [128, 512], F32)
                enb = sx("enb", [128, N], F32)
                NET = 6
                eT = [sx(f"eT{i}", [128, 512], BF16) for i in range(NET)]
                pT = None if PR else [sx(f"pT{i}", [128, 512], BF16) for i in range(NET)]
                rec = [sx(f"rec{i}", [128, 512 if PR else 256], F32) for i in range(2)]
                ATs = [sx(f"ATs{i}", [128, 512], BF16) for i in range(2)]
                rstd = sx("rstd", [128, N], F32)
                gtmp = [sx(f"gtmp{i}", [128, N], BF16) for i in range(2)]
                NFT = 1 if PR else 2
                ftmp = [sx(f"ftmp{i}", [128, 512], F32) for i in range(NFT)]
                xb16 = sx("xb16", [128, D], BF16)
                yout = [sx(f"yout{i}", [128, D], F32) for i in range(2)]
                ct = [bb, eb]
                ctk = ["bb", "eb"]
                if PR:
                    kT_st = sx("kT_st", [128, 4, 5 * 128], BF16)
                    v_st = sx("v_st", [128, 5, 4, 128], BF16)
                    kT_meta = sx("kT_meta", [128, 4, 16], BF16)
                    v_meta = sx("v_meta", [16, 4, 128], BF16)
                    xmT = sx("xmT", [128, 8, 16], BF16)
                    S32 = sx("S32", [128, 8, 128], F32)
                    Sbf = sx("Sbf", [128, 8, 128], BF16)
                    uc = [sx(f"uc{i}", [128, 514], F32) for i in range(2)]
                    metakv32 = kv32[0][0:16, :]
                    T.op("dve", lambda e: e.memset(S32[:], 0.0), writes=[("S32", h) for h in range(8)])
                    T.op("dve", lambda e: e.memset(Sbf[:], 0.0), writes=[("Sbf", h) for h in range(8)])
                    T.op("dve", lambda e: e.memset(uhalo[:], 0.0), writes=[("uhalo", m) for m in range(NFF)])
                    T.op("dve", lambda e: e.memset(kT_st[:, :, 0:128], 0.0), writes=[("kT", 0)])
                    T.op("dve", lambda e: e.memset(v_st[:, 0, :, :], 0.0), writes=[("v", 0)])
                    T.dma("pool", "xT0", xTb[0][:], xT_d[:, :, 0:512], writes=[("xTb", 0)])
                    ws_issue_upto(2)
                    T.dma("pool", "c2", rmask[:], rmask_d[:, :], writes=["rmask"])
                    T.dma("pool", "c9", xmT[:], xmT_d[:, :, :], writes=["xmT"])
                    T.dma("pool", "c0", masks[:], masks_d.rearrange("i p q -> p i q"), writes=["masks"])
                else:
                    kTn = sx("kTn", [128, 4, 128], BF16)
                    vn = sx("vn", [128, 256], BF16)
                    kTc = sx("kTc", [128, 16, 4, 144], BF16)
                    vwc = sx("vwc", [128, 16, 256], BF16)
                    vmc = sx("vmc", [16, 16, 256], BF16)
                    GS = 2
                    NG = 16 // GS
                    s0f = [sx(f"s0f{i}", [128, GS, 8, 128], F32) for i in range(2)]
                    s0b = [sx(f"s0b{i}", [128, GS, 8, 128], BF16) for i in range(2)]
                    uext = sx("uext", [128, NFF, 16, 10], F32)
                    cst = sx("cst", [128, NFF, 16, 2], F32)
                    rmask8 = sx("rmask8", [128, 128], BF16)
                    rowmask = sx("rowmask", [128, 16], F32)
                    kem = [sx(f"kem{i}", [128, 8, 128], BF16) for i in range(2)]
                    T.dma("pool", "xT0", xTb[0][:], xsT_d[:, :, :], writes=[("xTb", 0)])
                    T.dma("pool", "ck4", rmask8[:], rmask8_d[:, :], writes=["rmask8"])
                    T.dma("sp", "ck5", rowmask[:], rowmask_d[:, :], writes=["rowmask"])

                    def s0_load(g):
                        T.dma("sp", f"s0f{g % 2}", s0f[g % 2][:], s0_d[:, GS * g:GS * g + GS, :, :], writes=[("s0f", g % 2)])
                        T.op("pool", lambda e: e.tensor_copy(out=s0b[g % 2][:], in_=s0f[g % 2][:]), reads=[("s0f", g % 2)], writes=[("s0b", g % 2)])

                    def sample_late_loads():
                        T.dma("pool", "ck0", kTc[:], ckT_d[:, :, :, :], writes=["kTc"])
                        T.dma("pool", "ck1", vwc[:], cwv_d[:, :, :], writes=["vwc"])
                        T.dma("pool", "ck2", vmc[:], cmv_d[:, :, :], writes=["vmc"])
                        T.dma("sp", "ck3", cst[:], scT_d[:, :, :, :], writes=["cst"])
                        T.op("dve", lambda e: e.tensor_copy(out=uext[:, :, :, 0:2], in_=cst[:]), reads=["cst"], writes=["uext"])
                        T.dma("sp", "wk0", wks_d[:, 0:120 * 256], cwk_raw_d[:, 8 * 256:128 * 256])
                        T.dma("sp", "wk0", wvs_d[:, 0:120 * 256], cwv_raw_d[:, 8 * 256:128 * 256])
                        s0_load(0)
                        s0_load(1)

                def head_prep(h, s_, rmask_ap):
                    f_a = sg[s_][:]
                    T.op("act", lambda e: e.activation(out=logf[:], in_=f_a, func=AF.Ln, scale=oml[:, h:h + 1], bias=lb[:, h:h + 1]),
                         reads=[("sg", s_), "oml", "lb"], writes=["logf"])
                    T.op("dve", lambda e: e.tensor_scalar(out=kk[:], in0=f_a, scalar1=noml[:, h:h + 1], scalar2=oml[:, h:h + 1],
                                                          op0=ALU.mult, op1=ALU.add),
                         reads=[("sg", s_), "oml"], writes=["kk"])
                    T.op("dve", lambda e: e.tensor_tensor_scan(out=bb[:, 0:N], data0=rmask_ap, data1=logf[:], initial=0.0,
                                                               op0=ALU.mult, op1=ALU.add),
                         reads=["rmask", "rmask8", "logf"], writes=["bb"])
                    T.op("act", lambda e: e.activation(out=eb[:, 0:N], in_=bb[:, 0:N], func=AF.Exp), reads=["bb"], writes=["eb"])
                    T.op("act", lambda e: e.activation(out=enb[:], in_=bb[:, 0:N], func=AF.Exp, scale=-1.0), reads=["bb"], writes=["enb"])
                    T.op("dve", lambda e: e.tensor_tensor(out=q_dT[:, h, :], in0=qh[s_][:], in1=eb[:, 0:N], op=ALU.mult),
                         reads=[("qh", s_), "eb"], writes=[("q_dT", h)])
                    T.op("dve", lambda e: e.tensor_tensor(out=k_dT[:, h, :], in0=kk[:], in1=enb[:], op=ALU.mult),
                         reads=["kk", "enb"], writes=[("k_dT", h)])
                    ebv = eb[:, 0:N].rearrange("p (c t) -> p c t", t=CL)
                    T.op("dve", lambda e: e.tensor_tensor(out=enb[:].rearrange("p (c t) -> p c t", t=CL),
                                                          in0=enb[:].rearrange("p (c t) -> p c t", t=CL),
                                                          in1=ebv[:, :, CL - 1:CL].broadcast_to([128, NCK, CL]), op=ALU.mult),
                         reads=["enb", "eb"], writes=["enb"])
                    T.op("dve", lambda e: e.tensor_tensor(out=oaT[:, h, :], in0=enb[:], in1=kk[:], op=ALU.mult),
                         reads=["enb", "kk"], writes=[("oaT", h)])
                    T.op("act", lambda e: e.activation(out=dec[:, h, 0:NCK], in_=ebv[:, :, CL - 1], func=AF.Copy),
                         reads=["eb"], writes=[("dec", h)])

                def ke_transposes(heads=range(8)):
                    T.fence(X1T_KEYS, KE_KEYS)
                    for h in heads:
                        ph = nb()
                        for j in range(NTL):
                            T.op("pe", lambda e: e.transpose(ptv[ph][:, j * 128:(j + 1) * 128],
                                                             oaT[:, h, j * 128:(j + 1) * 128], ident[:]),
                                 reads=[("oaT", h), "ident"], writes=[("pb", ph)], inc=(j == NTL - 1))
                        T.op("act", lambda e: e.activation(out=ke[:, :, h, :],
                                                           in_=ptv[ph][:, 0:N].rearrange("p (j d) -> p j d", d=128),
                                                           func=AF.Copy),
                             reads=[("pb", ph)], writes=KE_KEYS)

                deferred = []
                if os.environ.get("KDEBUG"):
                    print("phase", mode, "sbuf bytes remaining", nc.sbuf_bytes_remaining)
                for st in (range(NST) if PR else [NST]):
                    xs = st % 2 if PR else 0
                    if PR and st + 1 < NST:
                        T.dma("pool", f"xT{(st + 1) % 2}", xTb[(st + 1) % 2][:], xT_d[:, :, (st + 1) * 512:(st + 2) * 512],
                              writes=[("xTb", (st + 1) % 2)])
                    xk = [("xTb", xs)]
                    warm = PR and st == 0
                    T.fence(H_KEYS, A_KEYS)

                    pend = {}
                    for ci in range(28):
                        kind, i = WIN_CHUNKS[ci]
                        if ci % 4 == 0:
                            wslot = ws_next(U_F0 + ci // 4)
                            T.prewait("pe", writes=[("pb", b) for b in peek_banks(4)])
                        asl = slice(384, 512) if (warm and kind == "q") else slice(0, N)
                        An = asl.stop - asl.start
                        bank = fm_chunk(wslot, ci % 4, lambda k: xTb[xs][:, k, asl], xk, An)
                        if kind == "q":
                            T.op("act", lambda e: e.activation(out=qT[:, i, asl], in_=pb[bank][:, 0:An], func=AF.Copy, scale=0.125),
                                 reads=[("pb", bank)], writes=["qT"])
                            if i % 2 == 0:
                                pend["hp"] = (i, pend["qh"])
                            else:
                                head_prep(pend["hp"][0], pend["hp"][1], rmask[:] if PR else rmask8[:])
                                head_prep(i, pend["qh"], rmask[:] if PR else rmask8[:])
                                if deferred:
                                    deferred.pop(0)()
                        elif kind == "kd":
                            if PR:
                                T.op("act", lambda e: e.activation(out=kT_st[:, i, 128:640], in_=pb[bank][:, :], func=AF.Copy),
                                     reads=[("pb", bank)], writes=[("kT", s) for s in range(1, 5)])
                                if st == 0:
                                    bm = fm_chunk(wslot, ci % 4, lambda k: xmT[:, k, :], ["xmT"], 16)
                                    T.op("act", lambda e: e.activation(out=kT_meta[:, i, :], in_=pb[bm][:, 0:16], func=AF.Copy),
                                         reads=[("pb", bm)], writes=["kT_meta"])
                            else:
                                T.op("act", lambda e: e.activation(out=kTn[:, i, :], in_=pb[bank][:, 0:N], func=AF.Copy),
                                     reads=[("pb", bank)], writes=["kTn"])
                        elif kind == "hq":
                            s_ = rot("qh", 2)
                            pend["qh"] = s_
                            T.op("act", lambda e: e.activation(out=qh[s_][:], in_=pb[bank][:, 0:N], func=AF.Sigmoid),
                                 reads=[("pb", bank)], writes=[("qh", s_)])
                            T.op("dve", lambda e: e.tensor_tensor(out=qh[s_][:], in0=pb[bank][:, 0:N], in1=qh[s_][:], op=ALU.mult),
                                 reads=[("pb", bank), ("qh", s_)], writes=[("qh", s_)])
                        elif kind == "hf":
                            s_ = pend["qh"]
                            T.op("act", lambda e: e.activation(out=sg[s_][:], in_=pb[bank][:, 0:N], func=AF.Sigmoid),
                                 reads=[("pb", bank)], writes=[("sg", s_)])
                    while deferred:
                        deferred.pop(0)()
                    if dbg == (st, "A"):
                        T.finish()
                        return True

                    if not PR:
                        sample_late_loads()
                    wkv = ws_next(U_KV)
                    if PR and st == 0:
                        bm = tm_group(wkv, lambda k: xmT[:, k, 0:16], ["xmT"], 512, M=16)
                        T.op("act", lambda e: e.activation(out=metakv32, in_=pb[bm][0:16, :], func=AF.Copy),
                             reads=[("pb", bm)], writes=[("kv32", 0)])
                        T.op("dve", lambda e: e.tensor_copy(out=v_meta[:].rearrange("p k (a d) -> p k a d", a=2),
                                                            in_=metakv32[:, 256:512].rearrange("p (k d) -> p k d", d=64).unsqueeze(2).broadcast_to([16, 4, 2, 64])),
                             reads=[("kv32", 0)], writes=["v_meta"])
                        T.dma("sp", "mkv", metakv_d[:, :], metakv32, reads=[("kv32", 0)])
                    for j in range(NTL):
                        bank = tm_group(wkv, lambda k: xTb[xs][:, k, j * 128:(j + 1) * 128], xk, 512)
                        ks_ = rot("kv", NKV)
                        T.op("act", lambda e: e.activation(out=kv32[ks_][:], in_=pb[bank][:, :], func=AF.Copy),
                             reads=[("pb", bank)], writes=[("kv32", ks_)])
                        if PR:
                            T.op("dve", lambda e: e.tensor_copy(out=v_st[:, j + 1, :, :].rearrange("p k (a d) -> p k a d", a=2),
                                                                in_=kv32[ks_][:, 256:512].rearrange("p (k d) -> p k d", d=64).unsqueeze(2).broadcast_to([128, 4, 2, 64])),
                                 reads=[("kv32", ks_)], writes=[("v", j + 1)])
                            if st == NST - 1 and j == 3:
                                T.dma("sp", "lkv", lastkv_d[:, :], kv32[ks_][:], reads=[("kv32", ks_)])
                        else:
                            T.op("dve", lambda e: e.tensor_copy(out=vn[:], in_=kv32[ks_][:, 256:512]),
                                 reads=[("kv32", ks_)], writes=["vn"])
                            for b in range(16):
                                T.dma("sp", "wk1", wks_d[b:b + 1, 120 * 256:128 * 256].rearrange("a (t c) -> (a t) c", c=256),
                                      kv32[ks_][b * 8:(b + 1) * 8, 0:256], reads=[("kv32", ks_)])
                                T.dma("sp", "wk1", wvs_d[b:b + 1, 120 * 256:128 * 256].rearrange("a (t c) -> (a t) c", c=256),
                                      kv32[ks_][b * 8:(b + 1) * 8, 256:512], reads=[("kv32", ks_)])
                    whi = {}

                    def hi_group(half, j):
                        if j == 0:
                            whi[half] = ws_next(U_HI0 + half)
                        bank = tm_group(whi[half], lambda k: xTb[xs][:, k, j * 128:(j + 1) * 128], xk, 512)
                        T.op("act", lambda e: e.activation(out=hv[:, j, half * 512:(half + 1) * 512], in_=pb[bank][:, :], func=AF.Copy),
                             reads=[("pb", bank)], writes=[("hv", j)])
                    fill_H = [(half, j) for half in range(2) for j in range(NTL)]
                    if not PR:
                        while fill_H:
                            hi_group(*fill_H.pop(0))
                        ke_transposes()
                    if dbg == (st, "B"):
                        T.finish()
                        return True

                    def perm(ap):
                        return ap if PR else ap.rearrange("p (c b t) -> p b c t", c=2, t=8)

                    def attn_finish(k, o_groups, d_groups, tsl):
                        bo = nb()
                        bd = nb()
                        for h2 in range(2):
                            n = len(o_groups)
                            for gi, g in enumerate(o_groups):
                                g(bo, h2, gi == 0, gi == n - 1, False)
                        for h2 in range(2):
                            for gi, g in enumerate(d_groups):
                                g(bd, h2, gi == 0, False, True)
                            T.op("pe", lambda e: e.matmul(pb[bd][h2 * 64:(h2 + 1) * 64, 0:256], lhsT=ones_bf[0:1, 0:64],
                                                          rhs=perm(sinkrow[0:1, k, h2 * 256:(h2 + 1) * 256]), start=False, stop=True),
                                 reads=["sinkrow", "ones"], writes=[("pb", bd)], inc=(h2 == 1))
                        rs_ = rot("rec", 2)
                        T.op("dve", lambda e: e.reciprocal(out=rec[rs_][:], in_=pb[bd][:, 0:256]),
                             reads=[("pb", bd)], writes=[("rec", rs_)])
                        if PR:
                            o_out = oaT[:, 2 * k:2 * k + 2, tsl]
                            o_in0 = pb[bo][:, 0:256].rearrange("p (c q) -> p c q", q=128)
                            o_in1 = rec[rs_][:].rearrange("p (c q) -> p c q", q=128)
                        else:
                            o_out = oaT[:, 2 * k:2 * k + 2, :].rearrange("p c (b t) -> p b c t", t=8)
                            o_in0 = pb[bo][:, 0:256].rearrange("p (b c t) -> p b c t", c=2, t=8)
                            o_in1 = rec[rs_][:].rearrange("p (b c t) -> p b c t", c=2, t=8)
                        T.op("dve", lambda e: e.tensor_tensor(out=o_out, in0=o_in0, in1=o_in1, op=ALU.mult),
                             reads=[("pb", bo), ("rec", rs_)], writes=[("oaT", 2 * k), ("oaT", 2 * k + 1)])

                    def attn_finish_pr(k, groups, tsl):
                        bo = nb()
                        bd = nb()
                        n = len(groups)
                        for gi, (nk, pap, pkey, vap, gkeys) in enumerate(groups):
                            T.op("pe", lambda e: e.matmul(pb[bo][:, :], lhsT=vap, rhs=pap[0:nk, :], start=(gi == 0), stop=(gi == n - 1)),
                                 reads=[pkey] + gkeys, writes=[("pb", bo)], inc=(gi == n - 1))
                        for gi, (nk, pap, pkey, vap, gkeys) in enumerate(groups):
                            T.op("pe", lambda e: e.matmul(pb[bd][:, :], lhsT=ones_bf[0:nk, :], rhs=pap[0:nk, :], start=(gi == 0), stop=False),
                                 reads=[pkey, "ones"], writes=[("pb", bd)], inc=False)
                        T.op("pe", lambda e: e.matmul(pb[bd][:, :], lhsT=ones_bf[0:1, :], rhs=sinkrow[0:1, k, :], start=False, stop=True),
                             reads=["sinkrow", "ones"], writes=[("pb", bd)])
                        rs_ = rot("rec", 2)
                        T.op("act", lambda e: e.activation(out=rec[rs_][:], in_=pb[bd][:, :], func=AF.Ln), reads=[("pb", bd)], writes=[("rec", rs_)])
                        T.op("act", lambda e: e.activation(out=rec[rs_][:], in_=rec[rs_][:], func=AF.Exp, scale=-1.0),
                             reads=[("rec", rs_)], writes=[("rec", rs_)])
                        for h2 in range(2):
                            ps_ = slice(h2 * 64, (h2 + 1) * 64)
                            cs_ = slice(h2 * 256, (h2 + 1) * 256)
                            T.op("dve", lambda e: e.tensor_tensor(out=oaT[ps_, 2 * k:2 * k + 2, tsl],
                                                                  in0=pb[bo][ps_, cs_].rearrange("p (c q) -> p c q", q=128),
                                                                  in1=rec[rs_][ps_, cs_].rearrange("p (c q) -> p c q", q=128), op=ALU.mult),
                                 reads=[("pb", bo), ("rec", rs_)], writes=[("oaT", 2 * k), ("oaT", 2 * k + 1)])

                    def dense_group(k, j, nk, kfn, vap, mi, gkeys):
                        es_ = rot("et", NET)
                        for h2 in range(2):
                            bank = nb()
                            T.op("pe", lambda e: e.matmul(pb[bank][0:nk, 0:256], lhsT=kfn(h2),
                                                          rhs=qT[h2 * 64:(h2 + 1) * 64, 2 * k:2 * k + 2, j * 128:(j + 1) * 128],
                                                          start=True, stop=True),
                                 reads=gkeys + ["qT"], writes=[("pb", bank)])
                            T.op("act", lambda e: e.activation(out=eT[es_][0:nk, h2 * 256:(h2 + 1) * 256], in_=pb[bank][0:nk, 0:256], func=AF.Exp),
                                 reads=[("pb", bank)], writes=[("eT", es_)])
                        if mi is not None and PR:
                            T.op("dve", lambda e: e.tensor_tensor(
                                out=eT[es_][0:nk, :].rearrange("p (a q) -> p a q", q=128),
                                in0=eT[es_][0:nk, :].rearrange("p (a q) -> p a q", q=128),
                                in1=masks[0:nk, mi, :].unsqueeze(1).broadcast_to([nk, 4, 128]), op=ALU.mult),
                                 reads=[("eT", es_), "masks"], writes=[("eT", es_)])
                            pap, pkey = eT[es_], ("eT", es_)
                        elif mi is not None:
                            T.op("dve", lambda e: e.tensor_tensor(
                                out=pT[es_][0:nk, :].rearrange("p (a q) -> p a q", q=128),
                                in0=eT[es_][0:nk, :].rearrange("p (a q) -> p a q", q=128),
                                in1=masks[0:nk, mi, :].unsqueeze(1).broadcast_to([nk, 4, 128]), op=ALU.mult),
                                 reads=[("eT", es_), "masks"], writes=[("pT", es_)])
                            pap, pkey = pT[es_], ("pT", es_)
                        else:
                            pap, pkey = eT[es_], ("eT", es_)
                        if PR:
                            return (nk, pap, pkey, vap, gkeys)

                        def emit(bank, h2, first, last, den):
                            lhsT = ones_bf[0:nk, 0:64] if den else vap
                            T.op("pe", lambda e: e.matmul(pb[bank][h2 * 64:(h2 + 1) * 64, 0:256], lhsT=lhsT,
                                                          rhs=perm(pap[0:nk, h2 * 256:(h2 + 1) * 256]), start=first, stop=last),
                                 reads=[pkey, "ones"] + gkeys, writes=[("pb", bank)], inc=(last and h2 == 1))
                        return emit

                    if PR:
                        it = 0
                        for j in ([3] if warm else range(4)):
                            gt = 4 * st + j
                            halo = (st == 0 and j == 3)
                            for k in range(4):
                                if it < 4:
                                    ke_transposes([2 * it, 2 * it + 1])
                                elif fill_H:
                                    hi_group(*fill_H.pop(0))
                                if warm and fill_H:
                                    hi_group(*fill_H.pop(0))
                                    hi_group(*fill_H.pop(0))
                                it += 1
                                gs = [dense_group(k, j, 16, lambda h2: kT_meta[h2 * 64:(h2 + 1) * 64, k, :],
                                                  v_meta[0:16, k, :], 5 if halo else None, ["kT_meta", "v_meta"])]
                                if gt > 0:
                                    mi = 3 if halo else (4 if (st == 1 and j == 0) else 1)
                                    gs.append(dense_group(k, j, 128, lambda h2: kT_st[h2 * 64:(h2 + 1) * 64, k, j * 128:(j + 1) * 128],
                                                          v_st[:, j, k, :], mi, [("kT", j), ("v", j)]))
                                gs.append(dense_group(k, j, 128, lambda h2: kT_st[h2 * 64:(h2 + 1) * 64, k, (j + 1) * 128:(j + 2) * 128],
                                                      v_st[:, j + 1, k, :], 2 if halo else 0, [("kT", j + 1), ("v", j + 1)]))
                                if pend.get("attn"):
                                    attn_finish_pr(*pend.pop("attn"))
                                pend["attn"] = (k, gs, slice(j * 128, (j + 1) * 128))
                        attn_finish_pr(*pend.pop("attn"))
                        while fill_H:
                            hi_group(*fill_H.pop(0))
                        if st + 1 < NST:
                            T.op("act", lambda e: e.activation(out=kT_st[:, :, 0:128], in_=kT_st[:, :, 512:640], func=AF.Copy),
                                 reads=[("kT", 4)], writes=[("kT", 0)])
                            T.op("act", lambda e: e.activation(out=v_st[:, 0, :, :], in_=v_st[:, 4, :, :], func=AF.Copy),
                                 reads=[("v", 4)], writes=[("v", 0)])
                    else:
                        T.fence(X1_KEYS, OH_KEYS)
                        HS = {}
                        def hs_pre():
                            bintra = []
                            for hh in range(2):
                                bA = nb()
                                for h4 in range(4):
                                    h = hh * 4 + h4
                                    T.op("pe", lambda e: e.matmul(pb[bA][:, h4 * 128:(h4 + 1) * 128], lhsT=k_dT[:, h, :], rhs=q_dT[:, h, :],
                                                                  start=True, stop=True),
                                         reads=[("k_dT", h), ("q_dT", h)], writes=[("pb", bA)], inc=(h4 == 3))
                                as_ = rot("ats", 2)
                                T.op("dve", lambda e: e.tensor_tensor(out=ATs[as_][:, :].rearrange("p (h t) -> p h t", t=128),
                                                                      in0=pb[bA][:, :].rearrange("p (h t) -> p h t", t=128),
                                                                      in1=masks[:, 6, :].unsqueeze(1).broadcast_to([128, 4, 128]), op=ALU.mult),
                                     reads=[("pb", bA), "masks"], writes=[("ATs", as_)])
                                bo = nb()
                                for h4 in range(4):
                                    h = hh * 4 + h4
                                    T.op("pe", lambda e: e.matmul(pb[bo][:, h4 * 128:(h4 + 1) * 128], lhsT=hv[:, 0, h * 128:(h + 1) * 128],
                                                                  rhs=ATs[as_][:, h4 * 128:(h4 + 1) * 128], start=True, stop=True),
                                         reads=[("hv", 0), ("ATs", as_)], writes=[("pb", bo)], inc=(h4 == 3))
                                fs = rot("f", 2)
                                T.op("act", lambda e: e.activation(out=ftmp[fs][:], in_=pb[bo][:, :], func=AF.Copy),
                                     reads=[("pb", bo)], writes=[("ftmp", fs)])
                                bintra.append(fs)
                            binter = [nb(), nb()]
                            reserved.update(binter)
                            return bintra, binter
                        def hs_group(g):
                            sl = g % 2
                            for bl in range(GS):
                                b = GS * g + bl
                                for h in range(8):
                                    T.op("pe", lambda e: e.matmul(pb[HS['binter'][h // 4]][:, (h % 4) * 128 + b * 8:(h % 4) * 128 + (b + 1) * 8],
                                                                  lhsT=s0b[sl][:, bl, h, :], rhs=q_dT[:, h, b * 8:(b + 1) * 8],
                                                                  start=True, stop=True),
                                         reads=[("s0b", sl), ("q_dT", h)], writes=[("pb", HS['binter'][h // 4])],
                                         inc=(h == 7))
                                km = rot("kem", 2)
                                T.op("dve", lambda e: e.tensor_scalar(out=kem[km][:], in0=ke[:, 0, :, :], scalar1=rowmask[:, b:b + 1], scalar2=None,
                                                                      op0=ALU.mult),
                                     reads=KE_KEYS + ["rowmask"], writes=[("kem", km)])
                                for hh in range(2):
                                    bc = nb()
                                    for h4 in range(4):
                                        h = hh * 4 + h4
                                        T.op("pe", lambda e: e.matmul(pb[bc][:, h4 * 128:(h4 + 1) * 128], lhsT=kem[km][:, h, :],
                                                                      rhs=hv[:, 0, h * 128:(h + 1) * 128], start=True, stop=True),
                                             reads=[("kem", km), ("hv", 0)], writes=[("pb", bc)], inc=(h4 == 3))
                                    for h4 in range(4):
                                        h = hh * 4 + h4
                                        T.op("dve", lambda e: e.scalar_tensor_tensor(out=s0f[sl][:, bl, h, :], in0=s0f[sl][:, bl, h, :],
                                                                                     scalar=dec[:, h, b:b + 1],
                                                                                     in1=pb[bc][:, h4 * 128:(h4 + 1) * 128], op0=ALU.mult, op1=ALU.add),
                                             reads=[("s0f", sl), ("dec", h), ("pb", bc)], writes=[("s0f", sl)])
                            T.dma("sp", f"hs{sl}", hs_d[GS * g:GS * g + GS].rearrange("b h k v -> k b h v"), s0f[sl][:], reads=[("s0f", sl)])
                            if g + 2 < NG:
                                s0_load(g + 2)
                        def hs_post(bintra, binter):
                            for hh in range(2):
                                T.op("dve", lambda e: e.tensor_tensor(out=ohT[:, hh * 4:hh * 4 + 4, :],
                                                                      in0=pb[binter[hh]][:, :].rearrange("p (h t) -> p h t", t=128),
                                                                      in1=ftmp[bintra[hh]][:].rearrange("p (h t) -> p h t", t=128), op=ALU.add),
                                     reads=[("pb", binter[hh]), ("ftmp", bintra[hh])], writes=[("ohT", hh * 4 + q) for q in range(4)])
                            reserved.difference_update(binter)
                        HS['bintra'], HS['binter'] = hs_pre()
                        for k in range(4):
                            g_new = dense_group(k, 0, 128, lambda h2: kTn[h2 * 64:(h2 + 1) * 64, k, :], vn[:, k * 64:(k + 1) * 64], 6,
                                                ["kTn", "vn"])
                            ew = rot("et", NET)
                            em = rot("et", NET)
                            for (nk, lo, e_i) in ((128, 16, ew), (16, 0, em)):
                                for h2 in range(2):
                                    bank = nb()
                                    for b in range(16):
                                        T.op("pe", lambda e: e.matmul(pb[bank][0:nk, b * 16:(b + 1) * 16],
                                                                      lhsT=kTc[h2 * 64:(h2 + 1) * 64, b, k, lo:lo + nk],
                                                                      rhs=qT[h2 * 64:(h2 + 1) * 64, 2 * k:2 * k + 2, b * 8:(b + 1) * 8],
                                                                      start=True, stop=True),
                                             reads=["kTc", "qT"], writes=[("pb", bank)], inc=(b == 15))
                                    T.op("act", lambda e: e.activation(out=eT[e_i][0:nk, h2 * 256:(h2 + 1) * 256], in_=pb[bank][0:nk, 0:256],
                                                                       func=AF.Exp),
                                         reads=[("pb", bank)], writes=[("eT", e_i)])
                            T.op("dve", lambda e: e.tensor_tensor(out=pT[ew][:, :].rearrange("p (a t) -> p a t", t=8),
                                                                  in0=eT[ew][:, :].rearrange("p (a t) -> p a t", t=8),
                                                                  in1=masks[:, 7, 0:8].unsqueeze(1).broadcast_to([128, 64, 8]), op=ALU.mult),
                                 reads=[("eT", ew), "masks"], writes=[("pT", ew)])

                            def cached(bank, h2, first, last, den, k=k, ew=ew, em=em):
                                if den:
                                    T.op("pe", lambda e: e.matmul(pb[bank][h2 * 64:(h2 + 1) * 64, 0:256], lhsT=ones_bf[:, 0:64],
                                                                  rhs=pT[ew][:, h2 * 256:(h2 + 1) * 256], start=False, stop=False),
                                         reads=[("pT", ew), "ones"], writes=[("pb", bank)], inc=False)
                                    T.op("pe", lambda e: e.matmul(pb[bank][h2 * 64:(h2 + 1) * 64, 0:256], lhsT=ones_bf[0:16, 0:64],
                                                                  rhs=eT[em][0:16, h2 * 256:(h2 + 1) * 256], start=False, stop=False),
                                         reads=[("eT", em), "ones"], writes=[("pb", bank)], inc=False)
                                    return
                                for b in range(16):
                                    outap = pb[bank][h2 * 64:(h2 + 1) * 64, b * 16:(b + 1) * 16]
                                    T.op("pe", lambda e: e.matmul(outap, lhsT=(ones_bf[:, 0:64] if den else vwc[:, b, k * 64:(k + 1) * 64]),
                                                                  rhs=pT[ew][:, h2 * 256 + b * 16: h2 * 256 + (b + 1) * 16],
                                                                  start=False, stop=False),
                                         reads=[("pT", ew), "vwc", "ones"], writes=[("pb", bank)], inc=False)
                                    T.op("pe", lambda e: e.matmul(outap, lhsT=(ones_bf[0:16, 0:64] if den else vmc[0:16, b, k * 64:(k + 1) * 64]),
                                                                  rhs=eT[em][0:16, h2 * 256 + b * 16: h2 * 256 + (b + 1) * 16],
                                                                  start=False, stop=(last and b == 15)),
                                         reads=[("eT", em), "vmc", "ones"], writes=[("pb", bank)], inc=(last and b == 15 and h2 == 1))
                            if pend.get("sattn"):
                                attn_finish(*pend.pop("sattn"))
                            pend["sattn"] = (k, [g_new, cached], [g_new, cached], slice(0, 128))
                            for g in range(k * NG // 4, (k + 1) * NG // 4):
                                hs_group(g)
                        attn_finish(*pend.pop("sattn"))
                        hs_post(HS['bintra'], HS['binter'])
                    if dbg == (st, "C"):
                        T.finish()
                        return True

                    T.fence(X1_KEYS, OH_KEYS)
                    if PR:
                        def hg_front(c):
                            hb = (c % 2) * 64
                            ts = slice(64 * c, 64 * c + 64)
                            if warm and c < 6:
                                return None
                            bA = nb()
                            for h in range(8):
                                T.op("pe", lambda e: e.matmul(pb[bA][hb:hb + 64, h * 64:(h + 1) * 64], lhsT=k_dT[:, h, ts], rhs=q_dT[:, h, ts],
                                                              start=True, stop=True),
                                     reads=[("k_dT", h), ("q_dT", h)], writes=[("pb", bA)], inc=(h == 7))
                            as_ = rot("ats", 2)
                            T.op("dve", lambda e: e.tensor_tensor(out=ATs[as_][hb:hb + 64, :].rearrange("p (h t) -> p h t", t=64),
                                                                  in0=pb[bA][hb:hb + 64, :].rearrange("p (h t) -> p h t", t=64),
                                                                  in1=masks[hb:hb + 64, 0, hb:hb + 64].unsqueeze(1).broadcast_to([64, 8, 64]), op=ALU.mult),
                                 reads=[("pb", bA), "masks"], writes=[("ATs", as_)])
                            return as_

                        def hg_upd_mm(c):
                            j = c // 2
                            hb = (c % 2) * 64
                            banks = []
                            for hh in range(2):
                                bc = nb()
                                for h4 in range(4):
                                    h = hh * 4 + h4
                                    T.op("pe", lambda e: e.matmul(pb[bc][:, h4 * 128:(h4 + 1) * 128], lhsT=ke[hb:hb + 64, j, h, :],
                                                                  rhs=hv[hb:hb + 64, j, h * 128:(h + 1) * 128], start=True, stop=True),
                                         reads=[("ke", j), ("hv", j)], writes=[("pb", bc)], inc=(h4 == 3))
                                banks.append(bc)
                            return banks

                        def hg_out(c, as_):
                            j = c // 2
                            hb = (c % 2) * 64
                            ts = slice(64 * c, 64 * c + 64)
                            if as_ is None:
                                if c == 0:
                                    T.op("dve", lambda e: e.memset(ohT[:, :, 0:384], 0.0), writes=OH_KEYS)
                                return
                            bo = nb()
                            for h in range(8):
                                T.op("pe", lambda e: e.matmul(pb[bo][:, h * 64:(h + 1) * 64], lhsT=hv[hb:hb + 64, j, h * 128:(h + 1) * 128],
                                                              rhs=ATs[as_][hb:hb + 64, h * 64:(h + 1) * 64], start=True, stop=False),
                                     reads=[("hv", j), ("ATs", as_)], writes=[("pb", bo)], inc=False)
                                T.op("pe", lambda e: e.matmul(pb[bo][:, h * 64:(h + 1) * 64], lhsT=Sbf[:, h, :], rhs=q_dT[:, h, ts],
                                                              start=False, stop=True),
                                     reads=[("Sbf", h), ("q_dT", h)], writes=[("pb", bo)], inc=(h == 7))
                            T.op("act", lambda e: e.activation(out=ohT[:, :, ts], in_=pb[bo][:, :].rearrange("p (h t) -> p h t", t=64), func=AF.Copy),
                                 reads=[("pb", bo)], writes=OH_KEYS)

                        def hg_upd(c, banks):
                            for hh in range(2):
                                bc = banks[hh]
                                for h4 in range(4):
                                    h = hh * 4 + h4
                                    T.op("dve", lambda e: e.scalar_tensor_tensor(out=S32[:, h, :], in0=S32[:, h, :], scalar=dec[:, h, c:c + 1],
                                                                                 in1=pb[bc][:, h4 * 128:(h4 + 1) * 128], op0=ALU.mult, op1=ALU.add),
                                         reads=[("S32", h), ("dec", h), ("pb", bc)], writes=[("S32", h)])
                                T.op("act", lambda e: e.activation(out=Sbf[:, hh * 4:hh * 4 + 4, :], in_=S32[:, hh * 4:hh * 4 + 4, :], func=AF.Copy),
                                     reads=[("S32", hh * 4 + q) for q in range(4)], writes=[("Sbf", hh * 4 + q) for q in range(4)])

                        fr = {0: hg_front(0)}
                        for c in range(8):
                            if c + 1 < 8:
                                fr[c + 1] = hg_front(c + 1)
                            bks = hg_upd_mm(c)
                            hg_out(c, fr[c])
                            hg_upd(c, bks)
                        if st == NST - 1:
                            T.dma("sp", "sfin", S_d[:, :, :], S32[:], reads=[("S32", h) for h in range(8)])
                    else:
                        pass
                    if dbg == (st, "D"):
                        T.finish()
                        return True

                    sq = qT
                    cs = slice(384, 512) if warm else slice(0, N)
                    Nc = cs.stop - cs.start
                    for h in range(8):
                        T.op("act", lambda e: e.activation(out=sq[:, h, cs], in_=ohT[:, h, cs], func=AF.Square),
                             reads=[("ohT", h)], writes=["qT"])
                    bank = nb()
                    for h in range(8):
                        T.op("pe", lambda e: e.matmul(pb[bank][:, 0:Nc], lhsT=ones_bf[:, :], rhs=sq[:, h, cs], start=(h == 0), stop=(h == 7)),
                             reads=["qT", "ones"], writes=[("pb", bank)], inc=(h == 7))
                    T.op("dve", lambda e: e.tensor_scalar(out=rstd[:, cs], in0=pb[bank][:, 0:Nc], scalar1=1.0 / 1024.0, scalar2=RMS_EPS,
                                                          op0=ALU.mult, op1=ALU.add),
                         reads=[("pb", bank)], writes=["rstd"])
                    T.op("pool", lambda e: e.tensor_tensor(out=rstd[:, cs], in0=rstd[:, cs], in1=mhalf[:, 0:1].broadcast_to([128, Nc]), op=ALU.pow),
                         reads=["rstd", "eps"], writes=["rstd"])
                    for ci in range(28, 52):
                        kind, i = WIN_CHUNKS[ci]
                        if ci % 4 == 0:
                            wslot = ws_next(U_F0 + ci // 4)
                            T.prewait("pe", writes=[("pb", b) for b in peek_banks(4)])
                        bank = fm_chunk(wslot, ci % 4, lambda k: xTb[xs][:, k, cs], xk, Nc)
                        gs_ = rot("g", 2)
                        fs = rot("f", NFT)
                        if kind == "hg":
                            T.op("act", lambda e: e.activation(out=gtmp[gs_][:, 0:Nc], in_=pb[bank][:, 0:Nc], func=AF.Silu),
                                 reads=[("pb", bank)], writes=[("gtmp", gs_)])
                            T.op("dve", lambda e: e.scalar_tensor_tensor(out=ftmp[fs][:, 0:Nc], in0=ohT[:, i, cs], scalar=normg[:, i:i + 1],
                                                                         in1=rstd[:, cs], op0=ALU.mult, op1=ALU.mult),
                                 reads=[("ohT", i), "normg", "rstd"], writes=[("ftmp", fs)])
                            T.op("dve", lambda e: e.tensor_tensor(out=ohT[:, i, cs], in0=ftmp[fs][:, 0:Nc], in1=gtmp[gs_][:, 0:Nc], op=ALU.mult),
                                 reads=[("ftmp", fs), ("gtmp", gs_)], writes=[("ohT", i)])
                        elif kind == "ga":
                            T.op("act", lambda e: e.activation(out=gtmp[gs_][:, 0:Nc], in_=pb[bank][:, 0:Nc], func=AF.Sigmoid),
                                 reads=[("pb", bank)], writes=[("gtmp", gs_)])
                            T.op("dve", lambda e: e.tensor_tensor(out=oaT[:, i, cs], in0=oaT[:, i, cs], in1=gtmp[gs_][:, 0:Nc], op=ALU.mult),
                                 reads=[("oaT", i), ("gtmp", gs_)], writes=[("oaT", i)])
                        else:
                            T.op("act", lambda e: e.activation(out=gtmp[gs_][:, 0:Nc], in_=pb[bank][:, 0:Nc], func=AF.Sigmoid),
                                 reads=[("pb", bank)], writes=[("gtmp", gs_)])
                            T.op("dve", lambda e: e.tensor_tensor(out=ftmp[fs][:, 0:Nc], in0=ohT[:, i, cs], in1=gtmp[gs_][:, 0:Nc], op=ALU.mult),
                                 reads=[("ohT", i), ("gtmp", gs_)], writes=[("ftmp", fs)])
                            T.op("dve", lambda e: e.tensor_tensor(out=oaT[:, i, cs], in0=oaT[:, i, cs], in1=ftmp[fs][:, 0:Nc], op=ALU.add),
                                 reads=[("oaT", i), ("ftmp", fs)], writes=[("oaT", i)])
                    if dbg == (st, "D2"):
                        T.finish()
                        return True

                    T.fence(OH_KEYS, X1_KEYS)
                    T.fence(KE_KEYS, X1T_KEYS)
                    wo = [ws_next(U_O0), ws_next(U_O1, hold=1)]
                    tiles = [3] if warm else list(range(NTL))

                    def e_mm(j):
                        banks = []
                        for chh in range(2):
                            bank = nb()
                            for k in range(8):
                                T.op("pe", lambda e: e.matmul(pb[bank][:, :], lhsT=oaT[:, k, j * 128:(j + 1) * 128], rhs=wbuf[wo[chh]][:, k, :],
                                                              start=(k == 0), stop=(k == 7)),
                                     reads=[("w", wo[chh]), ("oaT", k)], writes=[("pb", bank)], inc=(k == 7))
                            banks.append(bank)
                        return banks

                    def e_ln(j, banks):
                        xt_ = rot("xtok", 2)
                        if PR:
                            r0 = st * 512 + j * 128
                            T.dma("sp", f"xtok{xt_}", xtok[xt_][:], xtok_d[r0:r0 + 128, :], writes=[("xtok", xt_)])
                        else:
                            T.dma("sp", f"xtok{xt_}", xtok[xt_][:], xstok_d[:, :], writes=[("xtok", xt_)])
                        for chh in range(2):
                            T.op("dve", lambda e: e.scalar_tensor_tensor(out=x1[:, j, chh * 512:(chh + 1) * 512],
                                                                         in0=xtok[xt_][:, chh * 512:(chh + 1) * 512], scalar=ALPHA,
                                                                         in1=pb[banks[chh]][:, :], op0=ALU.mult, op1=ALU.add),
                                 reads=[("xtok", xt_), ("pb", banks[chh])], writes=[("x1", j)])
                        layer_norm(x1[:, j, :], ("x1", j), 0, x1[:, j, :], ("x1", j))
                        T.op("act", lambda e: e.activation(out=xb16[:], in_=x1[:, j, :], func=AF.Copy), reads=[("x1", j)], writes=["xb16"])

                    def e_tr(j):
                        for a in range(2):
                            ph = nb()
                            for q in range(4):
                                kc = a * 4 + q
                                T.op("pe", lambda e: e.transpose(ptv[ph][:, q * 128:(q + 1) * 128],
                                                                 xb16[:, kc * 128:(kc + 1) * 128], ident[:]),
                                     reads=["xb16", "ident"], writes=[("pb", ph)], inc=(q == 3))
                            T.op("act", lambda e: e.activation(out=x1T[:, a * 4:a * 4 + 4, j * 128:(j + 1) * 128],
                                                               in_=ptv[ph][:, 0:512].rearrange("p (q t) -> p q t", t=128),
                                                               func=AF.Copy),
                                 reads=[("pb", ph)], writes=["x1T"])

                    bk = e_mm(tiles[0])
                    e_ln(tiles[0], bk)
                    for ti in range(1, len(tiles)):
                        bk = e_mm(tiles[ti])
                        e_tr(tiles[ti - 1])
                        e_ln(tiles[ti], bk)
                    e_tr(tiles[-1])
                    if dbg == (st, "E"):
                        T.finish()
                        return True

                    T.fence(A_KEYS, H_KEYS)
                    for ci in range(24 if warm else 44):
                        if ci % 4 == 0:
                            wslot = ws_next((U_UPW0 if warm else U_UP0) + ci // 4)
                            T.prewait("pe", writes=[("pb", b) for b in peek_banks(4)])
                        if warm and ci >= NFF:
                            continue
                        kind, m = ("u", ci) if warm else UP_CHUNKS[ci]
                        if warm:
                            bank = fm_chunk(wslot, ci % 4, lambda k: x1T[:, k, 510:512], ["x1T"], 2)
                            T.op("act", lambda e: e.activation(out=uhalo[:, m, :], in_=pb[bank][:, 0:2], func=AF.Copy),
                                 reads=[("pb", bank)], writes=[("uhalo", m)])
                            continue
                        bank = fm_chunk(wslot, ci % 4, lambda k: x1T[:, k, :], ["x1T"], N)
                        if kind == "u":
                            us = rot("uc", 2)
                            if PR:
                                T.op("act", lambda e: e.activation(out=uc[us][:, 2:514], in_=pb[bank][:, :], func=AF.Copy),
                                     reads=[("pb", bank)], writes=[("uc", us)])
                                if pend.get("gelu"):
                                    pend.pop("gelu")()
                                T.op("dve", lambda e: e.tensor_copy(out=uc[us][:, 0:2], in_=uhalo[:, m, :]),
                                     reads=[("uhalo", m)], writes=[("uc", us)])
                                T.op("dve", lambda e: e.tensor_copy(out=uhalo[:, m, :], in_=uc[us][:, 512:514]),
                                     reads=[("uc", us)], writes=[("uhalo", m)])
                                u0, u1, u2 = uc[us][:, 0:512], uc[us][:, 1:513], uc[us][:, 2:514]
                                cto = ct[us][:, 0:512]
                                ukey = ("uc", us)
                            else:
                                T.op("act", lambda e: e.activation(out=uext[:, m, :, 2:10], in_=pb[bank][:, 0:N].rearrange("p (b t) -> p b t", t=8),
                                                                   func=AF.Copy),
                                     reads=[("pb", bank)], writes=["uext"])
                                if pend.get("gelu"):
                                    pend.pop("gelu")()
                                u0, u1, u2 = uext[:, m, :, 0:8], uext[:, m, :, 1:9], uext[:, m, :, 2:10]
                                cto = ct[us][:, 0:N].rearrange("p (b t) -> p b t", t=8)
                                ukey = "uext"
                            T.op("act", lambda e: e.activation(out=ct[us][:, 0:N], in_=pb[bank][:, 0:N], func=AF.Identity,
                                                               scale=cw[:, m, 2:3], bias=cb[:, m:m + 1]),
                                 reads=[("pb", bank), "cw", "cb"], writes=[ctk[us]])
                            T.op("dve", lambda e: e.scalar_tensor_tensor(out=cto, in0=u1, scalar=cw[:, m, 1:2], in1=cto,
                                                                         op0=ALU.mult, op1=ALU.add),
                                 reads=[ukey, "cw", ctk[us]], writes=[ctk[us]])
                            T.op("dve", lambda e: e.scalar_tensor_tensor(out=cto, in0=u0, scalar=cw[:, m, 0:1], in1=cto,
                                                                         op0=ALU.mult, op1=ALU.add),
                                 reads=[ukey, "cw", ctk[us]], writes=[ctk[us]])
                            def gelu(m=m, us=us):
                                T.op("act", lambda e: e.activation(out=hT[:, m, :], in_=ct[us][:, 0:N], func=AF.Gelu_apprx_tanh),
                                     reads=[ctk[us]], writes=[("hT", m)])
                            pend["gelu"] = gelu
                        else:
                            if pend.get("gelu"):
                                pend.pop("gelu")()
                            T.op("dve", lambda e: e.tensor_tensor(out=hT[:, m, :], in0=hT[:, m, :], in1=pb[bank][:, 0:N], op=ALU.mult),
                                 reads=[("hT", m), ("pb", bank)], writes=[("hT", m)])
                    if PR and st == NST - 1:
                        T.dma("sp", "convp", conv_d[:, :, :], uhalo[:], reads=[("uhalo", m) for m in range(NFF)])
                    if not PR:
                        T.op("dve", lambda e: e.tensor_copy(out=cst[:], in_=uext[:, :, :, 8:10]), reads=["uext"], writes=["cst"])
                        T.dma("sp", "convs", convs_d[:, :, :, :], cst[:], reads=["cst"])
                    if dbg == (st, "F"):
                        T.finish()
                        return True
                    if warm:
                        continue

                    for chh in range(2):
                        bj = [nb() for _ in range(NTL)]
                        for kg in range(3):
                            wd_ = ws_next(U_DN0 + chh * 3 + kg)
                            nk = 8 if kg < 2 else 6
                            for j in range(NTL):
                                for kq in range(nk):
                                    m = kg * 8 + kq
                                    T.op("pe", lambda e: e.matmul(pb[bj[j]][:, :], lhsT=hT[:, m, j * 128:(j + 1) * 128], rhs=wbuf[wd_][:, kq, :],
                                                                  start=(m == 0), stop=(m == NFF - 1)),
                                         reads=[("w", wd_), ("hT", m)], writes=[("pb", bj[j])], inc=(kq == nk - 1))
                        for j in range(NTL):
                            T.op("dve", lambda e: e.scalar_tensor_tensor(out=x1[:, j, chh * 512:(chh + 1) * 512],
                                                                         in0=x1[:, j, chh * 512:(chh + 1) * 512], scalar=ALPHA,
                                                                         in1=pb[bj[j]][:, :], op0=ALU.mult, op1=ALU.add),
                                 reads=[("x1", j), ("pb", bj[j])], writes=[("x1", j)])
                    for j in range(NTL):
                        def ln2(j=j, st=st):
                            ys_ = rot("y", 2)
                            layer_norm(x1[:, j, :], ("x1", j), 2, yout[ys_][:], ("yout", ys_))
                            if PR:
                                r0 = (st - 1) * 512 + j * 128
                                T.dma("sp", f"y{ys_}", y_d[r0:r0 + 128, :], yout[ys_][:], reads=[("yout", ys_)])
                            else:
                                T.dma("sp", f"y{ys_}", ys_d[:, :], yout[ys_][:], reads=[("yout", ys_)])
                        if PR and st < NST - 1:
                            deferred.append(ln2)
                        else:
                            ln2()
            return False

        if dbg == (0, "K"):
            T.finish()
            return nc
        if phase("prompt"):
            return nc
        if with_sample:
            T.barrier()
            if phase("sample"):
                return nc
        T.finish()
    return nc


def _host_consts(core):
    s = np.arange(128)[:, None]
    t = np.arange(128)[None, :]
    cur = (s <= t).astype(np.float32)
    prev = (s > t).astype(np.float32)
    zero = np.zeros((128, 128), np.float32)
    ones = np.ones((128, 128), np.float32)
    if core == 0:
        halo_cur, halo_prev, first_prev = zero, zero, zero
        halo_meta = (s <= (t - 112)).astype(np.float32)
    else:
        halo_cur, halo_prev, first_prev, halo_meta = cur, prev, prev, ones
    newk = ((s // 8 == t // 8) & (s <= t)).astype(np.float32)
    winm = (s > t).astype(np.float32)
    masks = np.stack([cur, prev, halo_cur, halo_prev, first_prev, halo_meta, newk, winm]).astype(np.float32)
    rmask = np.ones((128, 512), np.float32)
    rmask[:, 0::64] = 0.0
    rmask8 = np.ones((128, 128), np.float32)
    rmask8[:, 0::8] = 0.0
    rowmask = (np.arange(128)[:, None] // 8 == np.arange(16)[None, :]).astype(np.float32)
    return masks, rmask, rmask8, rowmask


def _fm(vec, n):
    return np.ascontiguousarray(vec.reshape(n, 128).T)


def kernel(**inputs):
    return _run(inputs, NSTM=4, with_sample=True)


def _run(inputs, NSTM, ncores=NCORES, with_sample=True, dbg=None):
    f32 = np.float32
    x_prompt = np.asarray(inputs["x_prompt"], f32)
    S = x_prompt.shape[1]
    assert S == ncores * NSTM * 512
    meta = np.asarray(inputs["meta_tokens"], f32)
    Gz = np.concatenate([np.zeros((384 + 112, D), f32), meta, x_prompt[0]], axis=0)
    NTOK = 512 * (NSTM + 1)
    W = build_weight_units(np.asarray(inputs["w_in"], f32), np.asarray(inputs["w_out"], f32),
                           np.asarray(inputs["w_up"], f32), np.asarray(inputs["w_down"], f32))
    lbraw = np.stack([_fm(np.asarray(inputs["hgrn_lb"], f32)[0], 8), _fm(np.asarray(inputs["hgrn_lb"], f32)[1], 8)], axis=1)
    normg = _fm(np.asarray(inputs["hgrn_norm_g"], f32)[0], 8)
    sinks = np.asarray(inputs["attn_sinks"], f32).reshape(1, 16)
    lnp = np.stack([np.asarray(inputs[k], f32)[0] for k in ("ln1_g", "ln1_b", "ln2_g", "ln2_b")])
    cwh = np.ascontiguousarray(np.asarray(inputs["conv_w"], f32)[0].reshape(3, NFF, 128).transpose(2, 1, 0))
    cbh = _fm(np.asarray(inputs["conv_b"], f32)[0], NFF)
    xmT = np.ascontiguousarray(meta.T.reshape(8, 128, 16).transpose(1, 0, 2))
    if with_sample:
        xsm = np.asarray(inputs["x_sample"], f32)
        cmk = np.asarray(inputs["cache_meta_k"], f32)[0]
        cmv = np.asarray(inputs["cache_meta_v"], f32)[0]
        cwk = np.asarray(inputs["cache_win_k"], f32)[0]
        cwv = np.asarray(inputs["cache_win_v"], f32)[0]
        shg = np.asarray(inputs["state_hgrn"], f32)[0]
        scv = np.asarray(inputs["state_conv"], f32)[0]
        NBS = xsm.shape[0] // ncores
        assert NBS == 16

    nc = build_program(NSTM, with_sample, dbg)
    in_maps = []
    for c in range(ncores):
        xc = Gz[c * NSTM * 512: c * NSTM * 512 + NTOK]
        masks, rmask, rmask8, rowmask = _host_consts(c)
        m = {
            "xT": np.ascontiguousarray(xc.T.reshape(8, 128, NTOK).transpose(1, 0, 2)),
            "xtok": np.ascontiguousarray(xc),
            "xmT": xmT, "W": W, "masks": masks, "rmask": rmask, "lbraw": lbraw, "normg": normg,
            "sinks": sinks, "lnp": lnp, "cw": cwh, "cb": cbh,
        }
        if with_sample:
            bs = slice(16 * c, 16 * c + 16)
            xs = xsm[bs].reshape(128, D)
            dup = lambda a: np.ascontiguousarray(np.concatenate([a, a], axis=0))
            m.update({
                "xsT": np.ascontiguousarray(xs.T.reshape(8, 128, 128).transpose(1, 0, 2)),
                "xstok": np.ascontiguousarray(xs),
                "ckT": dup(np.concatenate([cmk[bs].transpose(3, 0, 2, 1), cwk[bs].transpose(3, 0, 2, 1)], axis=3)),
                "cmv": np.ascontiguousarray(cmv[bs].transpose(1, 0, 2, 3).reshape(16, 16, 256)),
                "cwv": np.ascontiguousarray(cwv[bs].transpose(1, 0, 2, 3).reshape(128, 16, 256)),
                "cwk_raw": np.ascontiguousarray(cwk[bs].reshape(16, 128 * 256)),
                "cwv_raw": np.ascontiguousarray(cwv[bs].reshape(16, 128 * 256)),
                "s0": np.ascontiguousarray(shg[bs].transpose(2, 0, 1, 3)),
                "scT": np.ascontiguousarray(scv[bs].reshape(16, 2, NFF, 128).transpose(3, 2, 0, 1)),
                "rmask8": rmask8, "rowmask": rowmask,
            })
        in_maps.append(m)
    res = run_bass_kernel_spmd(nc, in_maps, core_ids=list(range(ncores)))
    R = res.results
    if dbg is not None:
        return R
    y_prompt = np.concatenate([R[c]["y"] for c in range(ncores)], axis=0)[None]
    mkv = R[0]["metakv"]
    meta_k = np.ascontiguousarray(mkv[:, 0:256]).reshape(1, 1, 16, 4, 64)
    meta_v = np.ascontiguousarray(mkv[:, 256:512]).reshape(1, 1, 16, 4, 64)
    lkv = R[ncores - 1]["lastkv"]
    win_k = np.ascontiguousarray(lkv[:, 0:256]).reshape(1, 1, 128, 4, 64)
    win_v = np.ascontiguousarray(lkv[:, 256:512]).reshape(1, 1, 128, 4, 64)
    hg = np.ascontiguousarray(R[ncores - 1]["Sfin"].transpose(1, 0, 2))[None, None]
    cv = R[ncores - 1]["convp"]
    conv_p = np.ascontiguousarray(cv.transpose(2, 1, 0).reshape(2, DFF))[None, None]
    if not with_sample:
        return (y_prompt, None, meta_k, meta_v, win_k, win_v, hg, conv_p)
    y_sample = np.concatenate([R[c]["ys"].reshape(16, 8, D) for c in range(ncores)], axis=0)
    wks = np.concatenate([R[c]["wks"].reshape(16, 128, 4, 64) for c in range(ncores)], axis=0)[None]
    wvs = np.concatenate([R[c]["wvs"].reshape(16, 128, 4, 64) for c in range(ncores)], axis=0)[None]
    hs = np.concatenate([R[c]["hs"] for c in range(ncores)], axis=0)[None]
    cs = np.concatenate([np.ascontiguousarray(R[c]["convs"].transpose(2, 3, 1, 0)).reshape(16, 2, DFF) for c in range(ncores)], axis=0)[None]
    f = lambda a: np.ascontiguousarray(a, dtype=np.float32)
    return tuple(f(a) for a in (y_prompt, y_sample, meta_k, meta_v, win_k, win_v, hg, conv_p, wks, wvs, hs, cs))
```

```python
import contextlib
import os
import numpy as np
import concourse.bass as bass
import concourse.mybir as mybir
from concourse.bass_utils import run_bass_kernel_spmd

F32 = mybir.dt.float32
BF16 = mybir.dt.bfloat16
AF = mybir.ActivationFunctionType
ALU = mybir.AluOpType
AX = mybir.AxisListType

NCORES = 8
D = 1024
DFF = 2816
NFF = 22
ALPHA = 2.0 ** 0.25
LN_EPS = 1e-5
RMS_EPS = 1e-6
NU_W = 41


class Trk:
    def __init__(self, nc, es):
        self.nc = nc
        self.es = es
        self.engs = {"pe": nc.tensor, "act": nc.scalar, "dve": nc.vector, "pool": nc.gpsimd, "sp": nc.sync}
        self.sem = {}
        self.cnt = {}
        for k in self.engs:
            self.sem[k] = es.enter_context(nc.semaphore("s_" + k))
            self.cnt[k] = 0
        self.waited = {k: {} for k in self.engs}
        self.lastw = {}
        self.readers = {}

    def _need(self, reads, writes):
        need = {}

        def add(k, c):
            if c > need.get(k, 0):
                need[k] = c
        for r in reads:
            lw = self.lastw.get(r)
            if lw:
                add(*lw)
        for w in writes:
            lw = self.lastw.get(w)
            if lw:
                add(*lw)
            for k, c in self.readers.get(w, {}).items():
                add(k, c)
        return need

    def _waits(self, e, need, skip_same=False):
        for k, c in need.items():
            if skip_same and k == e:
                continue
            if self.waited[e].get(k, 0) >= c:
                continue
            self.engs[e].wait_ge(self.sem[k], c)
            self.waited[e][k] = c

    def _record(self, key, c, reads, writes):
        for r in reads:
            d = self.readers.setdefault(r, {})
            if c > d.get(key, 0):
                d[key] = c
        for w in writes:
            self.lastw[w] = (key, c)
            self.readers[w] = {}

    def op(self, e, fn, reads=(), writes=(), inc=True):
        need = self._need(reads, writes)
        self._waits(e, need, skip_same=(e == "pe"))
        ins = fn(self.engs[e])
        if inc:
            self.cnt[e] += 1
            ins.then_inc(self.sem[e], 1)
            self._record(e, self.cnt[e], reads, writes)
        else:
            self._record(e, self.cnt[e] + 1, reads, writes)
        return ins

    def dma(self, q, semname, out, in_, reads=(), writes=(), **kw):
        key = "dma_" + semname
        if key not in self.sem:
            self.sem[key] = self.es.enter_context(self.nc.semaphore("sd_" + semname))
            self.cnt[key] = 0
        need = self._need(reads, writes)
        self._waits(q, need)
        ins = self.engs[q].dma_start(out=out, in_=in_, **kw)
        self.cnt[key] += 16
        ins.then_inc(self.sem[key], 16)
        self._record(key, self.cnt[key], reads, writes)
        return ins

    def prewait(self, e, writes=(), reads=()):
        self._waits(e, self._need(reads, writes), skip_same=(e == "pe"))

    def fence(self, src, dst):
        for dk in dst:
            d = self.readers.setdefault(dk, {})
            for sk in src:
                lw = self.lastw.get(sk)
                if lw and lw[1] > d.get(lw[0], 0):
                    d[lw[0]] = lw[1]
                for k, c in self.readers.get(sk, {}).items():
                    if c > d.get(k, 0):
                        d[k] = c

    def barrier(self):
        for e in self.engs:
            need = {k: c for k, c in self.cnt.items() if c > 0 and k != e}
            self._waits(e, need)

    def finish(self, e="sp"):
        for k in self.sem:
            if k.startswith("dma_") and self.cnt[k] > 0:
                if self.waited[e].get(k, 0) < self.cnt[k]:
                    self.engs[e].wait_ge(self.sem[k], self.cnt[k])


def _win_chunks():
    ch = []
    for h in range(8):
        ch += [("hq", h), ("hf", h), ("q", h)]
    ch += [("kd", i) for i in range(4)]
    ch += [("hg", h) for h in range(8)] + [("ga", i) for i in range(8)] + [("gb", h) for h in range(8)]
    return ch


WIN_CHUNKS = _win_chunks()
UP_CHUNKS = [("u", 0), ("u", 1)] + [c for m in range(2, NFF) for c in (("val", m - 2), ("u", m))] + \
            [("val", NFF - 2), ("val", NFF - 1)]
U_F0, U_KV, U_HI0, U_HI1, U_O0, U_O1, U_UP0, U_DN0, U_UPW0 = 0, 13, 14, 15, 16, 17, 18, 29, 35


def build_weight_units(w_in, w_out, w_up, w_down):
    W = np.zeros((NU_W, 128, 8, 512), np.float32)
    w_in = w_in[0]
    wq = w_in[:, 0:1024]; wk = w_in[:, 1024:1280]; wv = w_in[:, 1280:1536]
    whq = w_in[:, 1536:2560]; whf = w_in[:, 2560:3584]; whi = w_in[:, 3584:4608]
    whg = w_in[:, 4608:5632]; wga = w_in[:, 5632:6656]; wgb = w_in[:, 6656:7680]

    def kp(mat):
        return mat.reshape(8, 128, mat.shape[1]).transpose(1, 0, 2)
    src = {"q": wq, "hq": whq, "hf": whf, "hg": whg, "ga": wga, "gb": wgb}
    for ci, (kind, i) in enumerate(WIN_CHUNKS):
        if kind == "kd":
            blk = np.concatenate([wk[:, 64 * i:64 * i + 64]] * 2, axis=1)
        else:
            blk = src[kind][:, 128 * i:128 * i + 128]
        W[U_F0 + ci // 4, :, :, 128 * (ci % 4):128 * (ci % 4) + 128] = kp(blk)
    W[U_KV] = kp(np.concatenate([wk, wv], axis=1))
    W[U_HI0] = kp(whi[:, 0:512])
    W[U_HI1] = kp(whi[:, 512:1024])
    W[U_O0] = kp(w_out[0][:, 0:512])
    W[U_O1] = kp(w_out[0][:, 512:1024])
    wu = w_up[0]
    for ci, (kind, m) in enumerate(UP_CHUNKS):
        off = 0 if kind == "u" else DFF
        W[U_UP0 + ci // 4, :, :, 128 * (ci % 4):128 * (ci % 4) + 128] = kp(wu[:, off + 128 * m:off + 128 * m + 128])
    for m in range(NFF):
        W[U_UPW0 + m // 4, :, :, 128 * (m % 4):128 * (m % 4) + 128] = kp(wu[:, 128 * m:128 * m + 128])
    wd = w_down[0].reshape(NFF, 128, 1024).transpose(1, 0, 2)
    for chh in range(2):
        for kg in range(3):
            nk = 8 if kg < 2 else 6
            W[U_DN0 + chh * 3 + kg, :, 0:nk, :] = wd[:, kg * 8:kg * 8 + nk, chh * 512:(chh + 1) * 512]
    return W.reshape(NU_W, 128, 4096)


def build_program(NSTM=4, with_sample=True, dbg=None):
    NST = NSTM + 1
    NTOK = 512 * NST
    nc = bass.Bass("TRN2", target_bir_lowering=False)

    def din(name, shape, dt=F32):
        return nc.dram_tensor(name, shape, dt, kind="ExternalInput").ap()

    def dout(name, shape, dt=F32):
        return nc.dram_tensor(name, shape, dt, kind="ExternalOutput").ap()

    xT_d = din("xT", [128, 8, NTOK])
    xtok_d = din("xtok", [NTOK, D])
    xmT_d = din("xmT", [128, 8, 16])
    W_d = din("W", [NU_W, 128, 4096])
    masks_d = din("masks", [8, 128, 128])
    rmask_d = din("rmask", [128, 512])
    lbraw_d = din("lbraw", [128, 2, 8])
    normg_d = din("normg", [128, 8])
    sinks_d = din("sinks", [1, 16])
    ln_d = din("lnp", [4, D])
    cw_d = din("cw", [128, NFF, 3])
    cb_d = din("cb", [128, NFF])

    y_d = dout("y", [NSTM * 512, D])
    metakv_d = dout("metakv", [16, 512])
    lastkv_d = dout("lastkv", [128, 512])
    S_d = dout("Sfin", [128, 8, 128])
    conv_d = dout("convp", [128, NFF, 2])

    if with_sample:
        xsT_d = din("xsT", [128, 8, 128])
        xstok_d = din("xstok", [128, D])
        ckT_d = din("ckT", [128, 16, 4, 144])
        cmv_d = din("cmv", [16, 16, 256])
        cwv_d = din("cwv", [128, 16, 256])
        cwk_raw_d = din("cwk_raw", [16, 128 * 256])
        cwv_raw_d = din("cwv_raw", [16, 128 * 256])
        s0_d = din("s0", [128, 16, 8, 128])
        scT_d = din("scT", [128, NFF, 16, 2])
        rmask8_d = din("rmask8", [128, 128])
        rowmask_d = din("rowmask", [128, 16])
        ys_d = dout("ys", [128, D])
        wks_d = dout("wks", [16, 128 * 256])
        wvs_d = dout("wvs", [16, 128 * 256])
        hs_d = dout("hs", [16, 8, 128, 128])
        convs_d = dout("convs", [128, NFF, 16, 2])

    with contextlib.ExitStack() as es:
        T = Trk(nc, es)

        def sb(name, shape, dt):
            return es.enter_context(nc.sbuf_tensor("sb_" + name, shape, dt))

        NB = 8
        pb = [es.enter_context(nc.psum_tensor(f"pb{i}", [128, 512], F32)) for i in range(NB)]
        ptv = [pb[i][:, :].bitcast(BF16) for i in range(NB)]
        bank_ctr = [0]
        reserved = set()

        def nb():
            while True:
                b = bank_ctr[0] % NB
                bank_ctr[0] += 1
                if b not in reserved:
                    return b

        def peek_banks(n):
            out, c = [], bank_ctr[0]
            while len(out) < n:
                if (c % NB) not in reserved:
                    out.append(c % NB)
                c += 1
            return out
        pt_ctr = [0]

        def npt():
            h = pt_ctr[0] % 2
            pt_ctr[0] += 1
            return h

        ident = sb("ident", [128, 128], BF16)
        ones_bf = sb("ones_bf", [128, 128], BF16)
        masks = sb("masks", [128, 8, 128], BF16)
        rmask = sb("rmask", [128, 512], BF16)
        lbraw = sb("lbraw", [128, 2, 8], F32)
        lb = sb("lb", [128, 8], F32)
        oml = sb("oml", [128, 8], F32)
        noml = sb("noml", [128, 8], F32)
        normg = sb("normg", [128, 8], F32)
        sinks = sb("sinks", [1, 16], F32)
        sinkrow = sb("sinkrow", [1, 4, 512], BF16)
        lnp = sb("lnp", [128, 4, D], F32)
        cw = sb("cw", [128, NFF, 3], F32)
        cb = sb("cb", [128, NFF], F32)
        epsln = sb("epsln", [128, 1], F32)
        epsrms = sb("epsrms", [128, 1], F32)
        mhalf = sb("mhalf", [128, 1], F32)

        T.op("dve", lambda e: e.memset(ones_bf[:], 1.0), writes=["ones"])
        T.op("dve", lambda e: e.memset(epsln[:], LN_EPS), writes=["eps"])
        T.op("dve", lambda e: e.memset(epsrms[:], RMS_EPS), writes=["eps"])
        T.op("dve", lambda e: e.memset(mhalf[:], -0.5), writes=["eps"])
        T.op("pool", lambda e: e.memset(ident[:], 1.0), writes=["ident"])
        T.op("pool", lambda e: e.affine_select(out=ident[:], in_=ident[:], pattern=[[-1, 128]],
                                               compare_op=ALU.is_equal, fill=0.0, base=0, channel_multiplier=1),
             reads=["ident"], writes=["ident"])
        T.dma("sp", "c3", lbraw[:], lbraw_d[:, :, :], writes=["lbraw"])
        T.dma("sp", "c4", normg[:], normg_d[:, :], writes=["normg"])
        T.dma("sp", "c5", sinks[:], sinks_d[:, :], writes=["sinks"])
        for i in range(4):
            T.dma("sp", "c6", lnp[:, i, :], ln_d[i:i + 1, :].partition_broadcast(128).rearrange("p a d -> p (a d)"),
                  writes=["lnp"])
        T.dma("sp", "c7", cw[:], cw_d[:, :, :], writes=["cw"])
        T.dma("sp", "c8", cb[:], cb_d[:, :], writes=["cb"])
        T.op("dve", lambda e: e.tensor_tensor(out=lb[:], in0=lbraw[:, 0, :], in1=lbraw[:, 1, :], op=ALU.subtract),
             reads=["lbraw"], writes=["lb"])
        T.op("act", lambda e: e.activation(out=lb[:], in_=lb[:], func=AF.Sigmoid), reads=["lb"], writes=["lb"])
        T.op("dve", lambda e: e.tensor_scalar(out=oml[:], in0=lb[:], scalar1=-1.0, scalar2=1.0, op0=ALU.mult, op1=ALU.add),
             reads=["lb"], writes=["oml"])
        T.op("dve", lambda e: e.tensor_scalar(out=noml[:], in0=lb[:], scalar1=1.0, scalar2=-1.0, op0=ALU.mult, op1=ALU.add),
             reads=["lb"], writes=["oml"])
        T.op("act", lambda e: e.activation(out=sinks[:], in_=sinks[:], func=AF.Exp), reads=["sinks"], writes=["sinks"])
        srv = sinkrow[:].rearrange("p k (a b q) -> p k a b q", a=2, b=2)
        for k in range(4):
            for h2 in range(2):
                for c2 in range(2):
                    hd = 4 * k + 2 * c2 + h2
                    T.op("dve", lambda e: e.tensor_copy(out=srv[0:1, k, h2, c2, :],
                                                        in_=sinks[0:1, hd:hd + 1].broadcast_to([1, 128])),
                         reads=["sinks"], writes=["sinkrow"])

        NWB = 3
        wbuf = [sb(f"wbuf{i}", [128, 8, 512], BF16) for i in range(NWB)]
        sched = []
        for st in range(NST):
            units = list(range(0, 7)) + [U_KV, U_HI0, U_HI1] + list(range(7, 13)) + [U_O0, U_O1]
            if st == 0:
                units += list(range(U_UPW0, U_UPW0 + 6))
            else:
                units += list(range(U_UP0, U_UP0 + 11)) + list(range(U_DN0, U_DN0 + 6))
            sched += units
        if with_sample:
            sched += list(range(0, 7)) + [U_KV, U_HI0, U_HI1] + list(range(7, 13)) + [U_O0, U_O1] + \
                list(range(U_UP0, U_UP0 + 11)) + list(range(U_DN0, U_DN0 + 6))
        ws_state = {"issued": 0, "pos": 0}
        Wb_d = nc.dram_tensor("Wb16", [U_UPW0, 128, 4096], BF16, kind="Internal").ap()
        converted = set()
        pending_store = {}

        def ws_issue_upto(n):
            while ws_state["issued"] < min(n, len(sched)):
                i = ws_state["issued"]
                s = i % NWB
                u = sched[i]
                if u in converted:
                    T.dma("pool", f"w{s}", wbuf[s][:].rearrange("p k c -> p (k c)"), Wb_d[u], reads=[("Wb", u)], writes=[("w", s)])
                else:
                    T.dma("pool", f"w{s}", wbuf[s][:].rearrange("p k c -> p (k c)"), W_d[u], writes=[("w", s)])
                    if u < U_UPW0:
                        converted.add(u)
                        pending_store[i] = u
                ws_state["issued"] += 1

        def ws_next(expect, hold=0):
            i = ws_state["pos"]
            assert sched[i] == expect, (i, sched[i], expect)
            ws_issue_upto(i + 1)
            if i in pending_store:
                u = pending_store.pop(i)
                T.dma("pool", f"wst{i % NWB}", Wb_d[u], wbuf[i % NWB][:].rearrange("p k c -> p (k c)"),
                      reads=[("w", i % NWB)], writes=[("Wb", u)])
            ws_issue_upto(i + NWB - hold)
            ws_state["pos"] += 1
            return i % NWB

        stat6 = sb("stat6", [128, 2, 6], F32)
        mv = sb("mv", [128, 2], F32)
        rs1 = sb("rs1", [128, 1], F32)
        dec = sb("dec", [128, 8, 16], F32)
        uhalo = sb("uhalo", [128, NFF, 2], F32)
        rr = {}

        def rot(name, n):
            v = rr.get(name, 0) % n
            rr[name] = rr.get(name, 0) + 1
            return v

        def layer_norm(src_ap, src_key, gi, dst_ap, dst_key):
            for hh in range(2):
                T.op("dve", lambda e: e.bn_stats(out=stat6[:, hh, :], in_=src_ap[:, hh * 512:(hh + 1) * 512]),
                     reads=[src_key], writes=["stat6"])
            T.op("dve", lambda e: e.bn_aggr(out=mv[:], in_=stat6[:].rearrange("p a s -> p (a s)")),
                 reads=["stat6"], writes=["mv"])
            T.op("dve", lambda e: e.tensor_scalar(out=rs1[:], in0=mv[:, 1:2], scalar1=LN_EPS, scalar2=None, op0=ALU.add),
                 reads=["mv"], writes=["rs1"])
            T.op("pool", lambda e: e.tensor_tensor(out=rs1[:], in0=rs1[:], in1=mhalf[:], op=ALU.pow), reads=["rs1", "eps"], writes=["rs1"])
            T.op("dve", lambda e: e.tensor_scalar(out=src_ap, in0=src_ap, scalar1=mv[:, 0:1], scalar2=rs1[:],
                                                  op0=ALU.subtract, op1=ALU.mult),
                 reads=[src_key, "mv", "rs1"], writes=[src_key])
            T.op("dve", lambda e: e.tensor_tensor(out=src_ap, in0=src_ap, in1=lnp[:, gi, :], op=ALU.mult),
                 reads=[src_key, "lnp"], writes=[src_key])
            T.op("dve", lambda e: e.tensor_tensor(out=dst_ap, in0=src_ap, in1=lnp[:, gi + 1, :], op=ALU.add),
                 reads=[src_key, "lnp"], writes=[dst_key])

        def fm_chunk(wslot, ci4, rhs_of_k, rkeys, N):
            bank = nb()
            for k in range(8):
                T.op("pe", lambda e: e.matmul(pb[bank][:, 0:N], lhsT=wbuf[wslot][:, k, ci4 * 128:(ci4 + 1) * 128],
                                              rhs=rhs_of_k(k), start=(k == 0), stop=(k == 7)),
                     reads=[("w", wslot)] + rkeys, writes=[("pb", bank)], inc=(k == 7))
            return bank

        def tm_group(wslot, lhsT_of_k, lkeys, ncols, M=128):
            bank = nb()
            for k in range(8):
                T.op("pe", lambda e: e.matmul(pb[bank][0:M, 0:ncols], lhsT=lhsT_of_k(k), rhs=wbuf[wslot][:, k, 0:ncols],
                                              start=(k == 0), stop=(k == 7)),
                     reads=[("w", wslot)] + lkeys, writes=[("pb", bank)], inc=(k == 7))
            return bank

        def phase(mode):
            PR = (mode == "prompt")
            N = 512 if PR else 128
            NTL = N // 128
            CL = 64 if PR else 8
            NCK = N // CL
            with contextlib.ExitStack() as ex:
                def sx(name, shape, dt):
                    return ex.enter_context(nc.sbuf_tensor(("sp_" if PR else "ss_") + name, shape, dt))

                xTb = [sx(f"xTb{i}", [128, 8, N], BF16) for i in range(2 if PR else 1)]
                xtok = [sx(f"xtok{i}", [128, D], F32) for i in range(2)]
                bufA = sx("bufA", [128, 24 * N], BF16)
                qT = bufA[:, 0:8 * N].rearrange("p (k t) -> p k t", t=N)
                q_dT = bufA[:, 8 * N:16 * N].rearrange("p (k t) -> p k t", t=N)
                k_dT = bufA[:, 16 * N:24 * N].rearrange("p (k t) -> p k t", t=N)
                hT = bufA[:, 0:NFF * N].rearrange("p (k t) -> p k t", t=N)
                A_KEYS = ["qT"] + [("q_dT", h) for h in range(8)] + [("k_dT", h) for h in range(8)]
                H_KEYS = [("hT", m) for m in range(NFF)]
                hv = sx("hv", [128, NTL, 1024], BF16)
                bufC = sx("bufC", [128, 8 * N], BF16)
                ke = bufC[:].rearrange("p (j h d) -> p j h d", j=NTL, h=8)
                x1T = bufC[:].rearrange("p (k t) -> p k t", t=N)
                KE_KEYS = [("ke", j) for j in range(NTL)]
                X1T_KEYS = ["x1T"]
                oaT = sx("oaT", [128, 8, N], BF16)
                arB = sx("arB", [128, 8 * N], F32)
                ohT = arB[:].rearrange("p (h t) -> p h t", t=N)
                x1 = arB[:].rearrange("p (j d) -> p j d", d=D)
                OH_KEYS = [("ohT", h) for h in range(8)]
                X1_KEYS = [("x1", j) for j in range(NTL)]
                NKV = 1 if PR else 2
                kv32 = [sx(f"kv32_{i}", [128, 512], F32) for i in range(NKV)]
                qh = [sx(f"qh{i}", [128, N], F32) for i in range(2)]
                sg = [sx(f"sg{i}", [128, N], F32) for i in range(2)]
                logf = sx("logf", [128, N], F32)
                kk = sx("kk", [128, N], F32)
                bb = sx("bb", [128, 512], F32)
                eb = sx("eb", [128, 512], F32)
                enb = sx("enb", [128, N], F32)
                NET = 6
                eT = [sx(f"eT{i}", [128, 512], BF16) for i in range(NET)]
                pT = None if PR else [sx(f"pT{i}", [128, 512], BF16) for i in range(NET)]
                rec = [sx(f"rec{i}", [128, 512 if PR else 256], F32) for i in range(2)]
                ATs = [sx(f"ATs{i}", [128, 512], BF16) for i in range(2)]
                rstd = sx("rstd", [128, N], F32)
                gtmp = [sx(f"gtmp{i}", [128, N], BF16) for i in range(2)]
                NFT = 1 if PR else 2
                ftmp = [sx(f"ftmp{i}", [128, 512], F32) for i in range(NFT)]
                xb16 = sx("xb16", [128, D], BF16)
                yout = [sx(f"yout{i}", [128, D], F32) for i in range(2)]
                ct = [bb, eb]
                ctk = ["bb", "eb"]
                if PR:
                    kT_st = sx("kT_st", [128, 4, 5 * 128], BF16)
                    v_st = sx("v_st", [128, 5, 4, 128], BF16)
                    kT_meta = sx("kT_meta", [128, 4, 16], BF16)
                    v_meta = sx("v_meta", [16, 4, 128], BF16)
                    xmT = sx("xmT", [128, 8, 16], BF16)
                    S32 = sx("S32", [128, 8, 128], F32)
                    Sbf = sx("Sbf", [128, 8, 128], BF16)
                    uc = [sx(f"uc{i}", [128, 514], F32) for i in range(2)]
                    metakv32 = kv32[0][0:16, :]
                    T.op("dve", lambda e: e.memset(S32[:], 0.0), writes=[("S32", h) for h in range(8)])
                    T.op("dve", lambda e: e.memset(Sbf[:], 0.0), writes=[("Sbf", h) for h in range(8)])
                    T.op("dve", lambda e: e.memset(uhalo[:], 0.0), writes=[("uhalo", m) for m in range(NFF)])
                    T.op("dve", lambda e: e.memset(kT_st[:, :, 0:128], 0.0), writes=[("kT", 0)])
                    T.op("dve", lambda e: e.memset(v_st[:, 0, :, :], 0.0), writes=[("v", 0)])
                    T.dma("pool", "xT0", xTb[0][:], xT_d[:, :, 0:512], writes=[("xTb", 0)])
                    ws_issue_upto(2)
                    T.dma("pool", "c2", rmask[:], rmask_d[:, :], writes=["rmask"])
                    T.dma("pool", "c9", xmT[:], xmT_d[:, :, :], writes=["xmT"])
                    T.dma("pool", "c0", masks[:], masks_d.rearrange("i p q -> p i q"), writes=["masks"])
                else:
                    kTn = sx("kTn", [128, 4, 128], BF16)
                    vn = sx("vn", [128, 256], BF16)
                    kTc = sx("kTc", [128, 16, 4, 144], BF16)
                    vwc = sx("vwc", [128, 16, 256], BF16)
                    vmc = sx("vmc", [16, 16, 256], BF16)
                    GS = 2
                    NG = 16 // GS
                    s0f = [sx(f"s0f{i}", [128, GS, 8, 128], F32) for i in range(2)]
                    s0b = [sx(f"s0b{i}", [128, GS, 8, 128], BF16) for i in range(2)]
                    uext = sx("uext", [128, NFF, 16, 10], F32)
                    cst = sx("cst", [128, NFF, 16, 2], F32)
                    rmask8 = sx("rmask8", [128, 128], BF16)
                    rowmask = sx("rowmask", [128, 16], F32)
                    kem = [sx(f"kem{i}", [128, 8, 128], BF16) for i in range(2)]
                    T.dma("pool", "xT0", xTb[0][:], xsT_d[:, :, :], writes=[("xTb", 0)])
                    T.dma("pool", "ck4", rmask8[:], rmask8_d[:, :], writes=["rmask8"])
                    T.dma("sp", "ck5", rowmask[:], rowmask_d[:, :], writes=["rowmask"])

                    def s0_load(g):
                        T.dma("sp", f"s0f{g % 2}", s0f[g % 2][:], s0_d[:, GS * g:GS * g + GS, :, :], writes=[("s0f", g % 2)])
                        T.op("pool", lambda e: e.tensor_copy(out=s0b[g % 2][:], in_=s0f[g % 2][:]), reads=[("s0f", g % 2)], writes=[("s0b", g % 2)])

                    def sample_late_loads():
                        T.dma("pool", "ck0", kTc[:], ckT_d[:, :, :, :], writes=["kTc"])
                        T.dma("pool", "ck1", vwc[:], cwv_d[:, :, :], writes=["vwc"])
                        T.dma("pool", "ck2", vmc[:], cmv_d[:, :, :], writes=["vmc"])
                        T.dma("sp", "ck3", cst[:], scT_d[:, :, :, :], writes=["cst"])
                        T.op("dve", lambda e: e.tensor_copy(out=uext[:, :, :, 0:2], in_=cst[:]), reads=["cst"], writes=["uext"])
                        T.dma("sp", "wk0", wks_d[:, 0:120 * 256], cwk_raw_d[:, 8 * 256:128 * 256])
                        T.dma("sp", "wk0", wvs_d[:, 0:120 * 256], cwv_raw_d[:, 8 * 256:128 * 256])
                        s0_load(0)
                        s0_load(1)

                def head_prep(h, s_, rmask_ap):
                    f_a = sg[s_][:]
                    T.op("act", lambda e: e.activation(out=logf[:], in_=f_a, func=AF.Ln, scale=oml[:, h:h + 1], bias=lb[:, h:h + 1]),
                         reads=[("sg", s_), "oml", "lb"], writes=["logf"])
                    T.op("dve", lambda e: e.tensor_scalar(out=kk[:], in0=f_a, scalar1=noml[:, h:h + 1], scalar2=oml[:, h:h + 1],
                                                          op0=ALU.mult, op1=ALU.add),
                         reads=[("sg", s_), "oml"], writes=["kk"])
                    T.op("dve", lambda e: e.tensor_tensor_scan(out=bb[:, 0:N], data0=rmask_ap, data1=logf[:], initial=0.0,
                                                               op0=ALU.mult, op1=ALU.add),
                         reads=["rmask", "rmask8", "logf"], writes=["bb"])
                    T.op("act", lambda e: e.activation(out=eb[:, 0:N], in_=bb[:, 0:N], func=AF.Exp), reads=["bb"], writes=["eb"])
                    T.op("act", lambda e: e.activation(out=enb[:], in_=bb[:, 0:N], func=AF.Exp, scale=-1.0), reads=["bb"], writes=["enb"])
                    T.op("dve", lambda e: e.tensor_tensor(out=q_dT[:, h, :], in0=qh[s_][:], in1=eb[:, 0:N], op=ALU.mult),
                         reads=[("qh", s_), "eb"], writes=[("q_dT", h)])
                    T.op("dve", lambda e: e.tensor_tensor(out=k_dT[:, h, :], in0=kk[:], in1=enb[:], op=ALU.mult),
                         reads=["kk", "enb"], writes=[("k_dT", h)])
                    ebv = eb[:, 0:N].rearrange("p (c t) -> p c t", t=CL)
                    T.op("dve", lambda e: e.tensor_tensor(out=enb[:].rearrange("p (c t) -> p c t", t=CL),
                                                          in0=enb[:].rearrange("p (c t) -> p c t", t=CL),
                                                          in1=ebv[:, :, CL - 1:CL].broadcast_to([128, NCK, CL]), op=ALU.mult),
                         reads=["enb", "eb"], writes=["enb"])
                    T.op("dve", lambda e: e.tensor_tensor(out=oaT[:, h, :], in0=enb[:], in1=kk[:], op=ALU.mult),
                         reads=["enb", "kk"], writes=[("oaT", h)])
                    T.op("act", lambda e: e.activation(out=dec[:, h, 0:NCK], in_=ebv[:, :, CL - 1], func=AF.Copy),
                         reads=["eb"], writes=[("dec", h)])

                def ke_transposes(heads=range(8)):
                    T.fence(X1T_KEYS, KE_KEYS)
                    for h in heads:
                        ph = nb()
                        for j in range(NTL):
                            T.op("pe", lambda e: e.transpose(ptv[ph][:, j * 128:(j + 1) * 128],
                                                             oaT[:, h, j * 128:(j + 1) * 128], ident[:]),
                                 reads=[("oaT", h), "ident"], writes=[("pb", ph)], inc=(j == NTL - 1))
                        T.op("act", lambda e: e.activation(out=ke[:, :, h, :],
                                                           in_=ptv[ph][:, 0:N].rearrange("p (j d) -> p j d", d=128),
                                                           func=AF.Copy),
                             reads=[("pb", ph)], writes=KE_KEYS)

                deferred = []
                if os.environ.get("KDEBUG"):
                    print("phase", mode, "sbuf bytes remaining", nc.sbuf_bytes_remaining)
                for st in (range(NST) if PR else [NST]):
                    xs = st % 2 if PR else 0
                    if PR and st + 1 < NST:
                        T.dma("pool", f"xT{(st + 1) % 2}", xTb[(st + 1) % 2][:], xT_d[:, :, (st + 1) * 512:(st + 2) * 512],
                              writes=[("xTb", (st + 1) % 2)])
                    xk = [("xTb", xs)]
                    warm = PR and st == 0
                    T.fence(H_KEYS, A_KEYS)

                    pend = {}
                    for ci in range(28):
                        kind, i = WIN_CHUNKS[ci]
                        if ci % 4 == 0:
                            wslot = ws_next(U_F0 + ci // 4)
                            T.prewait("pe", writes=[("pb", b) for b in peek_banks(4)])
                        asl = slice(384, 512) if (warm and kind == "q") else slice(0, N)
                        An = asl.stop - asl.start
                        bank = fm_chunk(wslot, ci % 4, lambda k: xTb[xs][:, k, asl], xk, An)
                        if kind == "q":
                            T.op("act", lambda e: e.activation(out=qT[:, i, asl], in_=pb[bank][:, 0:An], func=AF.Copy, scale=0.125),
                                 reads=[("pb", bank)], writes=["qT"])
                            if i % 2 == 0:
                                pend["hp"] = (i, pend["qh"])
                            else:
                                head_prep(pend["hp"][0], pend["hp"][1], rmask[:] if PR else rmask8[:])
                                head_prep(i, pend["qh"], rmask[:] if PR else rmask8[:])
                                if deferred:
                                    deferred.pop(0)()
                        elif kind == "kd":
                            if PR:
                                T.op("act", lambda e: e.activation(out=kT_st[:, i, 128:640], in_=pb[bank][:, :], func=AF.Copy),
                                     reads=[("pb", bank)], writes=[("kT", s) for s in range(1, 5)])
                                if st == 0:
                                    bm = fm_chunk(wslot, ci % 4, lambda k: xmT[:, k, :], ["xmT"], 16)
                                    T.op("act", lambda e: e.activation(out=kT_meta[:, i, :], in_=pb[bm][:, 0:16], func=AF.Copy),
                                         reads=[("pb", bm)], writes=["kT_meta"])
                            else:
                                T.op("act", lambda e: e.activation(out=kTn[:, i, :], in_=pb[bank][:, 0:N], func=AF.Copy),
                                     reads=[("pb", bank)], writes=["kTn"])
                        elif kind == "hq":
                            s_ = rot("qh", 2)
                            pend["qh"] = s_
                            T.op("act", lambda e: e.activation(out=qh[s_][:], in_=pb[bank][:, 0:N], func=AF.Sigmoid),
                                 reads=[("pb", bank)], writes=[("qh", s_)])
                            T.op("dve", lambda e: e.tensor_tensor(out=qh[s_][:], in0=pb[bank][:, 0:N], in1=qh[s_][:], op=ALU.mult),
                                 reads=[("pb", bank), ("qh", s_)], writes=[("qh", s_)])
                        elif kind == "hf":
                            s_ = pend["qh"]
                            T.op("act", lambda e: e.activation(out=sg[s_][:], in_=pb[bank][:, 0:N], func=AF.Sigmoid),
                                 reads=[("pb", bank)], writes=[("sg", s_)])
                    while deferred:
                        deferred.pop(0)()
                    if dbg == (st, "A"):
                        T.finish()
                        return True

                    if not PR:
                        sample_late_loads()
                    wkv = ws_next(U_KV)
                    if PR and st == 0:
                        bm = tm_group(wkv, lambda k: xmT[:, k, 0:16], ["xmT"], 512, M=16)
                        T.op("act", lambda e: e.activation(out=metakv32, in_=pb[bm][0:16, :], func=AF.Copy),
                             reads=[("pb", bm)], writes=[("kv32", 0)])
                        T.op("dve", lambda e: e.tensor_copy(out=v_meta[:].rearrange("p k (a d) -> p k a d", a=2),
                                                            in_=metakv32[:, 256:512].rearrange("p (k d) -> p k d", d=64).unsqueeze(2).broadcast_to([16, 4, 2, 64])),
                             reads=[("kv32", 0)], writes=["v_meta"])
                        T.dma("sp", "mkv", metakv_d[:, :], metakv32, reads=[("kv32", 0)])
                    for j in range(NTL):
                        bank = tm_group(wkv, lambda k: xTb[xs][:, k, j * 128:(j + 1) * 128], xk, 512)
                        ks_ = rot("kv", NKV)
                        T.op("act", lambda e: e.activation(out=kv32[ks_][:], in_=pb[bank][:, :], func=AF.Copy),
                             reads=[("pb", bank)], writes=[("kv32", ks_)])
                        if PR:
                            T.op("dve", lambda e: e.tensor_copy(out=v_st[:, j + 1, :, :].rearrange("p k (a d) -> p k a d", a=2),
                                                                in_=kv32[ks_][:, 256:512].rearrange("p (k d) -> p k d", d=64).unsqueeze(2).broadcast_to([128, 4, 2, 64])),
                                 reads=[("kv32", ks_)], writes=[("v", j + 1)])
                            if st == NST - 1 and j == 3:
                                T.dma("sp", "lkv", lastkv_d[:, :], kv32[ks_][:], reads=[("kv32", ks_)])
                        else:
                            T.op("dve", lambda e: e.tensor_copy(out=vn[:], in_=kv32[ks_][:, 256:512]),
                                 reads=[("kv32", ks_)], writes=["vn"])
                            for b in range(16):
                                T.dma("sp", "wk1", wks_d[b:b + 1, 120 * 256:128 * 256].rearrange("a (t c) -> (a t) c", c=256),
                                      kv32[ks_][b * 8:(b + 1) * 8, 0:256], reads=[("kv32", ks_)])
                                T.dma("sp", "wk1", wvs_d[b:b + 1, 120 * 256:128 * 256].rearrange("a (t c) -> (a t) c", c=256),
                                      kv32[ks_][b * 8:(b + 1) * 8, 256:512], reads=[("kv32", ks_)])
                    whi = {}

                    def hi_group(half, j):
                        if j == 0:
                            whi[half] = ws_next(U_HI0 + half)
                        bank = tm_group(whi[half], lambda k: xTb[xs][:, k, j * 128:(j + 1) * 128], xk, 512)
                        T.op("act", lambda e: e.activation(out=hv[:, j, half * 512:(half + 1) * 512], in_=pb[bank][:, :], func=AF.Copy),
                             reads=[("pb", bank)], writes=[("hv", j)])
                    fill_H = [(half, j) for half in range(2) for j in range(NTL)]
                    if not PR:
                        while fill_H:
                            hi_group(*fill_H.pop(0))
                        ke_transposes()
                    if dbg == (st, "B"):
                        T.finish()
                        return True

                    def perm(ap):
                        return ap if PR else ap.rearrange("p (c b t) -> p b c t", c=2, t=8)

                    def attn_finish(k, o_groups, d_groups, tsl):
                        bo = nb()
                        bd = nb()
                        for h2 in range(2):
                            n = len(o_groups)
                            for gi, g in enumerate(o_groups):
                                g(bo, h2, gi == 0, gi == n - 1, False)
                        for h2 in range(2):
                            for gi, g in enumerate(d_groups):
                                g(bd, h2, gi == 0, False, True)
                            T.op("pe", lambda e: e.matmul(pb[bd][h2 * 64:(h2 + 1) * 64, 0:256], lhsT=ones_bf[0:1, 0:64],
                                                          rhs=perm(sinkrow[0:1, k, h2 * 256:(h2 + 1) * 256]), start=False, stop=True),
                                 reads=["sinkrow", "ones"], writes=[("pb", bd)], inc=(h2 == 1))
                        rs_ = rot("rec", 2)
                        T.op("dve", lambda e: e.reciprocal(out=rec[rs_][:], in_=pb[bd][:, 0:256]),
                             reads=[("pb", bd)], writes=[("rec", rs_)])
                        if PR:
                            o_out = oaT[:, 2 * k:2 * k + 2, tsl]
                            o_in0 = pb[bo][:, 0:256].rearrange("p (c q) -> p c q", q=128)
                            o_in1 = rec[rs_][:].rearrange("p (c q) -> p c q", q=128)
                        else:
                            o_out = oaT[:, 2 * k:2 * k + 2, :].rearrange("p c (b t) -> p b c t", t=8)
                            o_in0 = pb[bo][:, 0:256].rearrange("p (b c t) -> p b c t", c=2, t=8)
                            o_in1 = rec[rs_][:].rearrange("p (b c t) -> p b c t", c=2, t=8)
                        T.op("dve", lambda e: e.tensor_tensor(out=o_out, in0=o_in0, in1=o_in1, op=ALU.mult),
                             reads=[("pb", bo), ("rec", rs_)], writes=[("oaT", 2 * k), ("oaT", 2 * k + 1)])

                    def attn_finish_pr(k, groups, tsl):
                        bo = nb()
                        bd = nb()
                        n = len(groups)
                        for gi, (nk, pap, pkey, vap, gkeys) in enumerate(groups):
                            T.op("pe", lambda e: e.matmul(pb[bo][:, :], lhsT=vap, rhs=pap[0:nk, :], start=(gi == 0), stop=(gi == n - 1)),
                                 reads=[pkey] + gkeys, writes=[("pb", bo)], inc=(gi == n - 1))
                        for gi, (nk, pap, pkey, vap, gkeys) in enumerate(groups):
                            T.op("pe", lambda e: e.matmul(pb[bd][:, :], lhsT=ones_bf[0:nk, :], rhs=pap[0:nk, :], start=(gi == 0), stop=False),
                                 reads=[pkey, "ones"], writes=[("pb", bd)], inc=False)
                        T.op("pe", lambda e: e.matmul(pb[bd][:, :], lhsT=ones_bf[0:1, :], rhs=sinkrow[0:1, k, :], start=False, stop=True),
                             reads=["sinkrow", "ones"], writes=[("pb", bd)])
                        rs_ = rot("rec", 2)
                        T.op("act", lambda e: e.activation(out=rec[rs_][:], in_=pb[bd][:, :], func=AF.Ln), reads=[("pb", bd)], writes=[("rec", rs_)])
                        T.op("act", lambda e: e.activation(out=rec[rs_][:], in_=rec[rs_][:], func=AF.Exp, scale=-1.0),
                             reads=[("rec", rs_)], writes=[("rec", rs_)])
                        for h2 in range(2):
                            ps_ = slice(h2 * 64, (h2 + 1) * 64)
                            cs_ = slice(h2 * 256, (h2 + 1) * 256)
                            T.op("dve", lambda e: e.tensor_tensor(out=oaT[ps_, 2 * k:2 * k + 2, tsl],
                                                                  in0=pb[bo][ps_, cs_].rearrange("p (c q) -> p c q", q=128),
                                                                  in1=rec[rs_][ps_, cs_].rearrange("p (c q) -> p c q", q=128), op=ALU.mult),
                                 reads=[("pb", bo), ("rec", rs_)], writes=[("oaT", 2 * k), ("oaT", 2 * k + 1)])

                    def dense_group(k, j, nk, kfn, vap, mi, gkeys):
                        es_ = rot("et", NET)
                        for h2 in range(2):
                            bank = nb()
                            T.op("pe", lambda e: e.matmul(pb[bank][0:nk, 0:256], lhsT=kfn(h2),
                                                          rhs=qT[h2 * 64:(h2 + 1) * 64, 2 * k:2 * k + 2, j * 128:(j + 1) * 128],
                                                          start=True, stop=True),
                                 reads=gkeys + ["qT"], writes=[("pb", bank)])
                            T.op("act", lambda e: e.activation(out=eT[es_][0:nk, h2 * 256:(h2 + 1) * 256], in_=pb[bank][0:nk, 0:256], func=AF.Exp),
                                 reads=[("pb", bank)], writes=[("eT", es_)])
                        if mi is not None and PR:
                            T.op("dve", lambda e: e.tensor_tensor(
                                out=eT[es_][0:nk, :].rearrange("p (a q) -> p a q", q=128),
                                in0=eT[es_][0:nk, :].rearrange("p (a q) -> p a q", q=128),
                                in1=masks[0:nk, mi, :].unsqueeze(1).broadcast_to([nk, 4, 128]), op=ALU.mult),
                                 reads=[("eT", es_), "masks"], writes=[("eT", es_)])
                            pap, pkey = eT[es_], ("eT", es_)
                        elif mi is not None:
                            T.op("dve", lambda e: e.tensor_tensor(
                                out=pT[es_][0:nk, :].rearrange("p (a q) -> p a q", q=128),
                                in0=eT[es_][0:nk, :].rearrange("p (a q) -> p a q", q=128),
                                in1=masks[0:nk, mi, :].unsqueeze(1).broadcast_to([nk, 4, 128]), op=ALU.mult),
                                 reads=[("eT", es_), "masks"], writes=[("pT", es_)])
                            pap, pkey = pT[es_], ("pT", es_)
                        else:
                            pap, pkey = eT[es_], ("eT", es_)
                        if PR:
                            return (nk, pap, pkey, vap, gkeys)

                        def emit(bank, h2, first, last, den):
                            lhsT = ones_bf[0:nk, 0:64] if den else vap
                            T.op("pe", lambda e: e.matmul(pb[bank][h2 * 64:(h2 + 1) * 64, 0:256], lhsT=lhsT,
                                                          rhs=perm(pap[0:nk, h2 * 256:(h2 + 1) * 256]), start=first, stop=last),
                                 reads=[pkey, "ones"] + gkeys, writes=[("pb", bank)], inc=(last and h2 == 1))
                        return emit

                    if PR:
                        it = 0
                        for j in ([3] if warm else range(4)):
                            gt = 4 * st + j
                            halo = (st == 0 and j == 3)
                            for k in range(4):
                                if it < 4:
                                    ke_transposes([2 * it, 2 * it + 1])
                                elif fill_H:
                                    hi_group(*fill_H.pop(0))
                                if warm and fill_H:
                                    hi_group(*fill_H.pop(0))
                                    hi_group(*fill_H.pop(0))
                                it += 1
                                gs = [dense_group(k, j, 16, lambda h2: kT_meta[h2 * 64:(h2 + 1) * 64, k, :],
                                                  v_meta[0:16, k, :], 5 if halo else None, ["kT_meta", "v_meta"])]
                                if gt > 0:
                                    mi = 3 if halo else (4 if (st == 1 and j == 0) else 1)
                                    gs.append(dense_group(k, j, 128, lambda h2: kT_st[h2 * 64:(h2 + 1) * 64, k, j * 128:(j + 1) * 128],
                                                          v_st[:, j, k, :], mi, [("kT", j), ("v", j)]))
                                gs.append(dense_group(k, j, 128, lambda h2: kT_st[h2 * 64:(h2 + 1) * 64, k, (j + 1) * 128:(j + 2) * 128],
                                                      v_st[:, j + 1, k, :], 2 if halo else 0, [("kT", j + 1), ("v", j + 1)]))
                                if pend.get("attn"):
                                    attn_finish_pr(*pend.pop("attn"))
                                pend["attn"] = (k, gs, slice(j * 128, (j + 1) * 128))
                        attn_finish_pr(*pend.pop("attn"))
                        while fill_H:
                            hi_group(*fill_H.pop(0))
                        if st + 1 < NST:
                            T.op("act", lambda e: e.activation(out=kT_st[:, :, 0:128], in_=kT_st[:, :, 512:640], func=AF.Copy),
                                 reads=[("kT", 4)], writes=[("kT", 0)])
                            T.op("act", lambda e: e.activation(out=v_st[:, 0, :, :], in_=v_st[:, 4, :, :], func=AF.Copy),
                                 reads=[("v", 4)], writes=[("v", 0)])
                    else:
                        T.fence(X1_KEYS, OH_KEYS)
                        HS = {}
                        def hs_pre():
                            bintra = []
                            for hh in range(2):
                                bA = nb()
                                for h4 in range(4):
                                    h = hh * 4 + h4
                                    T.op("pe", lambda e: e.matmul(pb[bA][:, h4 * 128:(h4 + 1) * 128], lhsT=k_dT[:, h, :], rhs=q_dT[:, h, :],
                                                                  start=True, stop=True),
                                         reads=[("k_dT", h), ("q_dT", h)], writes=[("pb", bA)], inc=(h4 == 3))
                                as_ = rot("ats", 2)
                                T.op("dve", lambda e: e.tensor_tensor(out=ATs[as_][:, :].rearrange("p (h t) -> p h t", t=128),
                                                                      in0=pb[bA][:, :].rearrange("p (h t) -> p h t", t=128),
                                                                      in1=masks[:, 6, :].unsqueeze(1).broadcast_to([128, 4, 128]), op=ALU.mult),
                                     reads=[("pb", bA), "masks"], writes=[("ATs", as_)])
                                bo = nb()
                                for h4 in range(4):
                                    h = hh * 4 + h4
                                    T.op("pe", lambda e: e.matmul(pb[bo][:, h4 * 128:(h4 + 1) * 128], lhsT=hv[:, 0, h * 128:(h + 1) * 128],
                                                                  rhs=ATs[as_][:, h4 * 128:(h4 + 1) * 128], start=True, stop=True),
                                         reads=[("hv", 0), ("ATs", as_)], writes=[("pb", bo)], inc=(h4 == 3))
                                fs = rot("f", 2)
                                T.op("act", lambda e: e.activation(out=ftmp[fs][:], in_=pb[bo][:, :], func=AF.Copy),
                                     reads=[("pb", bo)], writes=[("ftmp", fs)])
                                bintra.append(fs)
                            binter = [nb(), nb()]
                            reserved.update(binter)
                            return bintra, binter
                        def hs_group(g):
                            sl = g % 2
                            for bl in range(GS):
                                b = GS * g + bl
                                for h in range(8):
                                    T.op("pe", lambda e: e.matmul(pb[HS['binter'][h // 4]][:, (h % 4) * 128 + b * 8:(h % 4) * 128 + (b + 1) * 8],
                                                                  lhsT=s0b[sl][:, bl, h, :], rhs=q_dT[:, h, b * 8:(b + 1) * 8],
                                                                  start=True, stop=True),
                                         reads=[("s0b", sl), ("q_dT", h)], writes=[("pb", HS['binter'][h // 4])],
                                         inc=(h == 7))
                                km = rot("kem", 2)
                                T.op("dve", lambda e: e.tensor_scalar(out=kem[km][:], in0=ke[:, 0, :, :], scalar1=rowmask[:, b:b + 1], scalar2=None,
                                                                      op0=ALU.mult),
                                     reads=KE_KEYS + ["rowmask"], writes=[("kem", km)])
                                for hh in range(2):
                                    bc = nb()
                                    for h4 in range(4):
                                        h = hh * 4 + h4
                                        T.op("pe", lambda e: e.matmul(pb[bc][:, h4 * 128:(h4 + 1) * 128], lhsT=kem[km][:, h, :],
                                                                      rhs=hv[:, 0, h * 128:(h + 1) * 128], start=True, stop=True),
                                             reads=[("kem", km), ("hv", 0)], writes=[("pb", bc)], inc=(h4 == 3))
                                    for h4 in range(4):
                                        h = hh * 4 + h4
                                        T.op("dve", lambda e: e.scalar_tensor_tensor(out=s0f[sl][:, bl, h, :], in0=s0f[sl][:, bl, h, :],
                                                                                     scalar=dec[:, h, b:b + 1],
                                                                                     in1=pb[bc][:, h4 * 128:(h4 + 1) * 128], op0=ALU.mult, op1=ALU.add),
                                             reads=[("s0f", sl), ("dec", h), ("pb", bc)], writes=[("s0f", sl)])
                            T.dma("sp", f"hs{sl}", hs_d[GS * g:GS * g + GS].rearrange("b h k v -> k b h v"), s0f[sl][:], reads=[("s0f", sl)])
                            if g + 2 < NG:
                                s0_load(g + 2)
                        def hs_post(bintra, binter):
                            for hh in range(2):
                                T.op("dve", lambda e: e.tensor_tensor(out=ohT[:, hh * 4:hh * 4 + 4, :],
                                                                      in0=pb[binter[hh]][:, :].rearrange("p (h t) -> p h t", t=128),
                                                                      in1=ftmp[bintra[hh]][:].rearrange("p (h t) -> p h t", t=128), op=ALU.add),
                                     reads=[("pb", binter[hh]), ("ftmp", bintra[hh])], writes=[("ohT", hh * 4 + q) for q in range(4)])
                            reserved.difference_update(binter)
                        HS['bintra'], HS['binter'] = hs_pre()
                        for k in range(4):
                            g_new = dense_group(k, 0, 128, lambda h2: kTn[h2 * 64:(h2 + 1) * 64, k, :], vn[:, k * 64:(k + 1) * 64], 6,
                                                ["kTn", "vn"])
                            ew = rot("et", NET)
                            em = rot("et", NET)
                            for (nk, lo, e_i) in ((128, 16, ew), (16, 0, em)):
                                for h2 in range(2):
                                    bank = nb()
                                    for b in range(16):
                                        T.op("pe", lambda e: e.matmul(pb[bank][0:nk, b * 16:(b + 1) * 16],
                                                                      lhsT=kTc[h2 * 64:(h2 + 1) * 64, b, k, lo:lo + nk],
                                                                      rhs=qT[h2 * 64:(h2 + 1) * 64, 2 * k:2 * k + 2, b * 8:(b + 1) * 8],
                                                                      start=True, stop=True),
                                             reads=["kTc", "qT"], writes=[("pb", bank)], inc=(b == 15))
                                    T.op("act", lambda e: e.activation(out=eT[e_i][0:nk, h2 * 256:(h2 + 1) * 256], in_=pb[bank][0:nk, 0:256],
                                                                       func=AF.Exp),
                                         reads=[("pb", bank)], writes=[("eT", e_i)])
                            T.op("dve", lambda e: e.tensor_tensor(out=pT[ew][:, :].rearrange("p (a t) -> p a t", t=8),
                                                                  in0=eT[ew][:, :].rearrange("p (a t) -> p a t", t=8),
                                                                  in1=masks[:, 7, 0:8].unsqueeze(1).broadcast_to([128, 64, 8]), op=ALU.mult),
                                 reads=[("eT", ew), "masks"], writes=[("pT", ew)])

                            def cached(bank, h2, first, last, den, k=k, ew=ew, em=em):
                                if den:
                                    T.op("pe", lambda e: e.matmul(pb[bank][h2 * 64:(h2 + 1) * 64, 0:256], lhsT=ones_bf[:, 0:64],
                                                                  rhs=pT[ew][:, h2 * 256:(h2 + 1) * 256], start=False, stop=False),
                                         reads=[("pT", ew), "ones"], writes=[("pb", bank)], inc=False)
                                    T.op("pe", lambda e: e.matmul(pb[bank][h2 * 64:(h2 + 1) * 64, 0:256], lhsT=ones_bf[0:16, 0:64],
                                                                  rhs=eT[em][0:16, h2 * 256:(h2 + 1) * 256], start=False, stop=False),
                                         reads=[("eT", em), "ones"], writes=[("pb", bank)], inc=False)
                                    return
                                for b in range(16):
                                    outap = pb[bank][h2 * 64:(h2 + 1) * 64, b * 16:(b + 1) * 16]
                                    T.op("pe", lambda e: e.matmul(outap, lhsT=(ones_bf[:, 0:64] if den else vwc[:, b, k * 64:(k + 1) * 64]),
                                                                  rhs=pT[ew][:, h2 * 256 + b * 16: h2 * 256 + (b + 1) * 16],
                                                                  start=False, stop=False),
                                         reads=[("pT", ew), "vwc", "ones"], writes=[("pb", bank)], inc=False)
                                    T.op("pe", lambda e: e.matmul(outap, lhsT=(ones_bf[0:16, 0:64] if den else vmc[0:16, b, k * 64:(k + 1) * 64]),
                                                                  rhs=eT[em][0:16, h2 * 256 + b * 16: h2 * 256 + (b + 1) * 16],
                                                                  start=False, stop=(last and b == 15)),
                                         reads=[("eT", em), "vmc", "ones"], writes=[("pb", bank)], inc=(last and b == 15 and h2 == 1))
                            if pend.get("sattn"):
                                attn_finish(*pend.pop("sattn"))
                            pend["sattn"] = (k, [g_new, cached], [g_new, cached], slice(0, 128))
                            for g in range(k * NG // 4, (k + 1) * NG // 4):
                                hs_group(g)
                        attn_finish(*pend.pop("sattn"))
                        hs_post(HS['bintra'], HS['binter'])
                    if dbg == (st, "C"):
                        T.finish()
                        return True

                    T.fence(X1_KEYS, OH_KEYS)
                    if PR:
                        def hg_front(c):
                            hb = (c % 2) * 64
                            ts = slice(64 * c, 64 * c + 64)
                            if warm and c < 6:
                                return None
                            bA = nb()
                            for h in range(8):
                                T.op("pe", lambda e: e.matmul(pb[bA][hb:hb + 64, h * 64:(h + 1) * 64], lhsT=k_dT[:, h, ts], rhs=q_dT[:, h, ts],
                                                              start=True, stop=True),
                                     reads=[("k_dT", h), ("q_dT", h)], writes=[("pb", bA)], inc=(h == 7))
                            as_ = rot("ats", 2)
                            T.op("dve", lambda e: e.tensor_tensor(out=ATs[as_][hb:hb + 64, :].rearrange("p (h t) -> p h t", t=64),
                                                                  in0=pb[bA][hb:hb + 64, :].rearrange("p (h t) -> p h t", t=64),
                                                                  in1=masks[hb:hb + 64, 0, hb:hb + 64].unsqueeze(1).broadcast_to([64, 8, 64]), op=ALU.mult),
                                 reads=[("pb", bA), "masks"], writes=[("ATs", as_)])
                            return as_

                        def hg_upd_mm(c):
                            j = c // 2
                            hb = (c % 2) * 64
                            banks = []
                            for hh in range(2):
                                bc = nb()
                                for h4 in range(4):
                                    h = hh * 4 + h4
                                    T.op("pe", lambda e: e.matmul(pb[bc][:, h4 * 128:(h4 + 1) * 128], lhsT=ke[hb:hb + 64, j, h, :],
                                                                  rhs=hv[hb:hb + 64, j, h * 128:(h + 1) * 128], start=True, stop=True),
                                         reads=[("ke", j), ("hv", j)], writes=[("pb", bc)], inc=(h4 == 3))
                                banks.append(bc)
                            return banks

                        def hg_out(c, as_):
                            j = c // 2
                            hb = (c % 2) * 64
                            ts = slice(64 * c, 64 * c + 64)
                            if as_ is None:
                                if c == 0:
                                    T.op("dve", lambda e: e.memset(ohT[:, :, 0:384], 0.0), writes=OH_KEYS)
                                return
                            bo = nb()
                            for h in range(8):
                                T.op("pe", lambda e: e.matmul(pb[bo][:, h * 64:(h + 1) * 64], lhsT=hv[hb:hb + 64, j, h * 128:(h + 1) * 128],
                                                              rhs=ATs[as_][hb:hb + 64, h * 64:(h + 1) * 64], start=True, stop=False),
                                     reads=[("hv", j), ("ATs", as_)], writes=[("pb", bo)], inc=False)
                                T.op("pe", lambda e: e.matmul(pb[bo][:, h * 64:(h + 1) * 64], lhsT=Sbf[:, h, :], rhs=q_dT[:, h, ts],
                                                              start=False, stop=True),
                                     reads=[("Sbf", h), ("q_dT", h)], writes=[("pb", bo)], inc=(h == 7))
                            T.op("act", lambda e: e.activation(out=ohT[:, :, ts], in_=pb[bo][:, :].rearrange("p (h t) -> p h t", t=64), func=AF.Copy),
                                 reads=[("pb", bo)], writes=OH_KEYS)

                        def hg_upd(c, banks):
                            for hh in range(2):
                                bc = banks[hh]
                                for h4 in range(4):
                                    h = hh * 4 + h4
                                    T.op("dve", lambda e: e.scalar_tensor_tensor(out=S32[:, h, :], in0=S32[:, h, :], scalar=dec[:, h, c:c + 1],
                                                                                 in1=pb[bc][:, h4 * 128:(h4 + 1) * 128], op0=ALU.mult, op1=ALU.add),
                                         reads=[("S32", h), ("dec", h), ("pb", bc)], writes=[("S32", h)])
                                T.op("act", lambda e: e.activation(out=Sbf[:, hh * 4:hh * 4 + 4, :], in_=S32[:, hh * 4:hh * 4 + 4, :], func=AF.Copy),
                                     reads=[("S32", hh * 4 + q) for q in range(4)], writes=[("Sbf", hh * 4 + q) for q in range(4)])

                        fr = {0: hg_front(0)}
                        for c in range(8):
                            if c + 1 < 8:
                                fr[c + 1] = hg_front(c + 1)
                            bks = hg_upd_mm(c)
                            hg_out(c, fr[c])
                            hg_upd(c, bks)
                        if st == NST - 1:
                            T.dma("sp", "sfin", S_d[:, :, :], S32[:], reads=[("S32", h) for h in range(8)])
                    else:
                        pass
                    if dbg == (st, "D"):
                        T.finish()
                        return True

                    sq = qT
                    cs = slice(384, 512) if warm else slice(0, N)
                    Nc = cs.stop - cs.start
                    for h in range(8):
                        T.op("act", lambda e: e.activation(out=sq[:, h, cs], in_=ohT[:, h, cs], func=AF.Square),
                             reads=[("ohT", h)], writes=["qT"])
                    bank = nb()
                    for h in range(8):
                        T.op("pe", lambda e: e.matmul(pb[bank][:, 0:Nc], lhsT=ones_bf[:, :], rhs=sq[:, h, cs], start=(h == 0), stop=(h == 7)),
                             reads=["qT", "ones"], writes=[("pb", bank)], inc=(h == 7))
                    T.op("act", lambda e: e.activation(out=rstd[:, cs], in_=pb[bank][:, 0:Nc], func=AF.Ln, bias=epsrms[:], scale=1.0 / 1024.0),
                         reads=[("pb", bank), "eps"], writes=["rstd"])
                    T.op("act", lambda e: e.activation(out=rstd[:, cs], in_=rstd[:, cs], func=AF.Exp, scale=-0.5), reads=["rstd"], writes=["rstd"])
                    for ci in range(28, 52):
                        kind, i = WIN_CHUNKS[ci]
                        if ci % 4 == 0:
                            wslot = ws_next(U_F0 + ci // 4)
                            T.prewait("pe", writes=[("pb", b) for b in peek_banks(4)])
                        bank = fm_chunk(wslot, ci % 4, lambda k: xTb[xs][:, k, cs], xk, Nc)
                        gs_ = rot("g", 2)
                        fs = rot("f", NFT)
                        if kind == "hg":
                            T.op("act", lambda e: e.activation(out=gtmp[gs_][:, 0:Nc], in_=pb[bank][:, 0:Nc], func=AF.Silu),
                                 reads=[("pb", bank)], writes=[("gtmp", gs_)])
                            T.op("dve", lambda e: e.scalar_tensor_tensor(out=ftmp[fs][:, 0:Nc], in0=ohT[:, i, cs], scalar=normg[:, i:i + 1],
                                                                         in1=rstd[:, cs], op0=ALU.mult, op1=ALU.mult),
                                 reads=[("ohT", i), "normg", "rstd"], writes=[("ftmp", fs)])
                            T.op("dve", lambda e: e.tensor_tensor(out=ohT[:, i, cs], in0=ftmp[fs][:, 0:Nc], in1=gtmp[gs_][:, 0:Nc], op=ALU.mult),
                                 reads=[("ftmp", fs), ("gtmp", gs_)], writes=[("ohT", i)])
                        elif kind == "ga":
                            T.op("act", lambda e: e.activation(out=gtmp[gs_][:, 0:Nc], in_=pb[bank][:, 0:Nc], func=AF.Sigmoid),
                                 reads=[("pb", bank)], writes=[("gtmp", gs_)])
                            T.op("dve", lambda e: e.tensor_tensor(out=oaT[:, i, cs], in0=oaT[:, i, cs], in1=gtmp[gs_][:, 0:Nc], op=ALU.mult),
                                 reads=[("oaT", i), ("gtmp", gs_)], writes=[("oaT", i)])
                        else:
                            T.op("act", lambda e: e.activation(out=gtmp[gs_][:, 0:Nc], in_=pb[bank][:, 0:Nc], func=AF.Sigmoid),
                                 reads=[("pb", bank)], writes=[("gtmp", gs_)])
                            T.op("dve", lambda e: e.tensor_tensor(out=ftmp[fs][:, 0:Nc], in0=ohT[:, i, cs], in1=gtmp[gs_][:, 0:Nc], op=ALU.mult),
                                 reads=[("ohT", i), ("gtmp", gs_)], writes=[("ftmp", fs)])
                            T.op("dve", lambda e: e.tensor_tensor(out=oaT[:, i, cs], in0=oaT[:, i, cs], in1=ftmp[fs][:, 0:Nc], op=ALU.add),
                                 reads=[("oaT", i), ("ftmp", fs)], writes=[("oaT", i)])
                    if dbg == (st, "D2"):
                        T.finish()
                        return True

                    T.fence(OH_KEYS, X1_KEYS)
                    T.fence(KE_KEYS, X1T_KEYS)
                    wo = [ws_next(U_O0), ws_next(U_O1, hold=1)]
                    tiles = [3] if warm else list(range(NTL))

                    def e_mm(j):
                        banks = []
                        for chh in range(2):
                            bank = nb()
                            for k in range(8):
                                T.op("pe", lambda e: e.matmul(pb[bank][:, :], lhsT=oaT[:, k, j * 128:(j + 1) * 128], rhs=wbuf[wo[chh]][:, k, :],
                                                              start=(k == 0), stop=(k == 7)),
                                     reads=[("w", wo[chh]), ("oaT", k)], writes=[("pb", bank)], inc=(k == 7))
                            banks.append(bank)
                        return banks

                    def e_ln(j, banks):
                        xt_ = rot("xtok", 2)
                        if PR:
                            r0 = st * 512 + j * 128
                            T.dma("sp", f"xtok{xt_}", xtok[xt_][:], xtok_d[r0:r0 + 128, :], writes=[("xtok", xt_)])
                        else:
                            T.dma("sp", f"xtok{xt_}", xtok[xt_][:], xstok_d[:, :], writes=[("xtok", xt_)])
                        for chh in range(2):
                            T.op("dve", lambda e: e.scalar_tensor_tensor(out=x1[:, j, chh * 512:(chh + 1) * 512],
                                                                         in0=xtok[xt_][:, chh * 512:(chh + 1) * 512], scalar=ALPHA,
                                                                         in1=pb[banks[chh]][:, :], op0=ALU.mult, op1=ALU.add),
                                 reads=[("xtok", xt_), ("pb", banks[chh])], writes=[("x1", j)])
                        layer_norm(x1[:, j, :], ("x1", j), 0, x1[:, j, :], ("x1", j))
                        T.op("act", lambda e: e.activation(out=xb16[:], in_=x1[:, j, :], func=AF.Copy), reads=[("x1", j)], writes=["xb16"])

                    def e_tr(j):
                        for a in range(2):
                            ph = nb()
                            for q in range(4):
                                kc = a * 4 + q
                                T.op("pe", lambda e: e.transpose(ptv[ph][:, q * 128:(q + 1) * 128],
                                                                 xb16[:, kc * 128:(kc + 1) * 128], ident[:]),
                                     reads=["xb16", "ident"], writes=[("pb", ph)], inc=(q == 3))
                            T.op("act", lambda e: e.activation(out=x1T[:, a * 4:a * 4 + 4, j * 128:(j + 1) * 128],
                                                               in_=ptv[ph][:, 0:512].rearrange("p (q t) -> p q t", t=128),
                                                               func=AF.Copy),
                                 reads=[("pb", ph)], writes=["x1T"])

                    bk = e_mm(tiles[0])
                    e_ln(tiles[0], bk)
                    for ti in range(1, len(tiles)):
                        bk = e_mm(tiles[ti])
                        e_tr(tiles[ti - 1])
                        e_ln(tiles[ti], bk)
                    e_tr(tiles[-1])
                    if dbg == (st, "E"):
                        T.finish()
                        return True

                    T.fence(A_KEYS, H_KEYS)
                    for ci in range(24 if warm else 44):
                        if ci % 4 == 0:
                            wslot = ws_next((U_UPW0 if warm else U_UP0) + ci // 4)
                            T.prewait("pe", writes=[("pb", b) for b in peek_banks(4)])
                        if warm and ci >= NFF:
                            continue
                        kind, m = ("u", ci) if warm else UP_CHUNKS[ci]
                        if warm:
                            bank = fm_chunk(wslot, ci % 4, lambda k: x1T[:, k, 510:512], ["x1T"], 2)
                            T.op("act", lambda e: e.activation(out=uhalo[:, m, :], in_=pb[bank][:, 0:2], func=AF.Copy),
                                 reads=[("pb", bank)], writes=[("uhalo", m)])
                            continue
                        bank = fm_chunk(wslot, ci % 4, lambda k: x1T[:, k, :], ["x1T"], N)
                        if kind == "u":
                            us = rot("uc", 2)
                            if PR:
                                T.op("act", lambda e: e.activation(out=uc[us][:, 2:514], in_=pb[bank][:, :], func=AF.Copy),
                                     reads=[("pb", bank)], writes=[("uc", us)])
                                if pend.get("gelu"):
                                    pend.pop("gelu")()
                                T.op("dve", lambda e: e.tensor_copy(out=uc[us][:, 0:2], in_=uhalo[:, m, :]),
                                     reads=[("uhalo", m)], writes=[("uc", us)])
                                T.op("dve", lambda e: e.tensor_copy(out=uhalo[:, m, :], in_=uc[us][:, 512:514]),
                                     reads=[("uc", us)], writes=[("uhalo", m)])
                                u0, u1, u2 = uc[us][:, 0:512], uc[us][:, 1:513], uc[us][:, 2:514]
                                cto = ct[us][:, 0:512]
                                ukey = ("uc", us)
                            else:
                                T.op("act", lambda e: e.activation(out=uext[:, m, :, 2:10], in_=pb[bank][:, 0:N].rearrange("p (b t) -> p b t", t=8),
                                                                   func=AF.Copy),
                                     reads=[("pb", bank)], writes=["uext"])
                                if pend.get("gelu"):
                                    pend.pop("gelu")()
                                u0, u1, u2 = uext[:, m, :, 0:8], uext[:, m, :, 1:9], uext[:, m, :, 2:10]
                                cto = ct[us][:, 0:N].rearrange("p (b t) -> p b t", t=8)
                                ukey = "uext"
                            T.op("act", lambda e: e.activation(out=ct[us][:, 0:N], in_=pb[bank][:, 0:N], func=AF.Identity,
                                                               scale=cw[:, m, 2:3], bias=cb[:, m:m + 1]),
                                 reads=[("pb", bank), "cw", "cb"], writes=[ctk[us]])
                            T.op("dve", lambda e: e.scalar_tensor_tensor(out=cto, in0=u1, scalar=cw[:, m, 1:2], in1=cto,
                                                                         op0=ALU.mult, op1=ALU.add),
                                 reads=[ukey, "cw", ctk[us]], writes=[ctk[us]])
                            T.op("dve", lambda e: e.scalar_tensor_tensor(out=cto, in0=u0, scalar=cw[:, m, 0:1], in1=cto,
                                                                         op0=ALU.mult, op1=ALU.add),
                                 reads=[ukey, "cw", ctk[us]], writes=[ctk[us]])
                            def gelu(m=m, us=us):
                                T.op("act", lambda e: e.activation(out=hT[:, m, :], in_=ct[us][:, 0:N], func=AF.Gelu_apprx_tanh),
                                     reads=[ctk[us]], writes=[("hT", m)])
                            pend["gelu"] = gelu
                        else:
                            if pend.get("gelu"):
                                pend.pop("gelu")()
                            T.op("dve", lambda e: e.tensor_tensor(out=hT[:, m, :], in0=hT[:, m, :], in1=pb[bank][:, 0:N], op=ALU.mult),
                                 reads=[("hT", m), ("pb", bank)], writes=[("hT", m)])
                    if PR and st == NST - 1:
                        T.dma("sp", "convp", conv_d[:, :, :], uhalo[:], reads=[("uhalo", m) for m in range(NFF)])
                    if not PR:
                        T.op("dve", lambda e: e.tensor_copy(out=cst[:], in_=uext[:, :, :, 8:10]), reads=["uext"], writes=["cst"])
                        T.dma("sp", "convs", convs_d[:, :, :, :], cst[:], reads=["cst"])
                    if dbg == (st, "F"):
                        T.finish()
                        return True
                    if warm:
                        continue

                    for chh in range(2):
                        bj = [nb() for _ in range(NTL)]
                        for kg in range(3):
                            wd_ = ws_next(U_DN0 + chh * 3 + kg)
                            nk = 8 if kg < 2 else 6
                            for j in range(NTL):
                                for kq in range(nk):
                                    m = kg * 8 + kq
                                    T.op("pe", lambda e: e.matmul(pb[bj[j]][:, :], lhsT=hT[:, m, j * 128:(j + 1) * 128], rhs=wbuf[wd_][:, kq, :],
                                                                  start=(m == 0), stop=(m == NFF - 1)),
                                         reads=[("w", wd_), ("hT", m)], writes=[("pb", bj[j])], inc=(kq == nk - 1))
                        for j in range(NTL):
                            T.op("dve", lambda e: e.scalar_tensor_tensor(out=x1[:, j, chh * 512:(chh + 1) * 512],
                                                                         in0=x1[:, j, chh * 512:(chh + 1) * 512], scalar=ALPHA,
                                                                         in1=pb[bj[j]][:, :], op0=ALU.mult, op1=ALU.add),
                                 reads=[("x1", j), ("pb", bj[j])], writes=[("x1", j)])
                    for j in range(NTL):
                        def ln2(j=j, st=st):
                            ys_ = rot("y", 2)
                            layer_norm(x1[:, j, :], ("x1", j), 2, yout[ys_][:], ("yout", ys_))
                            if PR:
                                r0 = (st - 1) * 512 + j * 128
                                T.dma("sp", f"y{ys_}", y_d[r0:r0 + 128, :], yout[ys_][:], reads=[("yout", ys_)])
                            else:
                                T.dma("sp", f"y{ys_}", ys_d[:, :], yout[ys_][:], reads=[("yout", ys_)])
                        if PR and st < NST - 1:
                            deferred.append(ln2)
                        else:
                            ln2()
            return False

        if dbg == (0, "K"):
            T.finish()
            return nc
        if phase("prompt"):
            return nc
        if with_sample:
            T.barrier()
            if phase("sample"):
                return nc
        T.finish()
    return nc


def _host_consts(core):
    s = np.arange(128)[:, None]
    t = np.arange(128)[None, :]
    cur = (s <= t).astype(np.float32)
    prev = (s > t).astype(np.float32)
    zero = np.zeros((128, 128), np.float32)
    ones = np.ones((128, 128), np.float32)
    if core == 0:
        halo_cur, halo_prev, first_prev = zero, zero, zero
        halo_meta = (s <= (t - 112)).astype(np.float32)
    else:
        halo_cur, halo_prev, first_prev, halo_meta = cur, prev, prev, ones
    newk = ((s // 8 == t // 8) & (s <= t)).astype(np.float32)
    winm = (s > t).astype(np.float32)
    masks = np.stack([cur, prev, halo_cur, halo_prev, first_prev, halo_meta, newk, winm]).astype(np.float32)
    rmask = np.ones((128, 512), np.float32)
    rmask[:, 0::64] = 0.0
    rmask8 = np.ones((128, 128), np.float32)
    rmask8[:, 0::8] = 0.0
    rowmask = (np.arange(128)[:, None] // 8 == np.arange(16)[None, :]).astype(np.float32)
    return masks, rmask, rmask8, rowmask


def _fm(vec, n):
    return np.ascontiguousarray(vec.reshape(n, 128).T)


def kernel(**inputs):
    return _run(inputs, NSTM=4, with_sample=True)


def _run(inputs, NSTM, ncores=NCORES, with_sample=True, dbg=None):
    f32 = np.float32
    x_prompt = np.asarray(inputs["x_prompt"], f32)
    S = x_prompt.shape[1]
    assert S == ncores * NSTM * 512
    meta = np.asarray(inputs["meta_tokens"], f32)
    Gz = np.concatenate([np.zeros((384 + 112, D), f32), meta, x_prompt[0]], axis=0)
    NTOK = 512 * (NSTM + 1)
    W = build_weight_units(np.asarray(inputs["w_in"], f32), np.asarray(inputs["w_out"], f32),
                           np.asarray(inputs["w_up"], f32), np.asarray(inputs["w_down"], f32))
    lbraw = np.stack([_fm(np.asarray(inputs["hgrn_lb"], f32)[0], 8), _fm(np.asarray(inputs["hgrn_lb"], f32)[1], 8)], axis=1)
    normg = _fm(np.asarray(inputs["hgrn_norm_g"], f32)[0], 8)
    sinks = np.asarray(inputs["attn_sinks"], f32).reshape(1, 16)
    lnp = np.stack([np.asarray(inputs[k], f32)[0] for k in ("ln1_g", "ln1_b", "ln2_g", "ln2_b")])
    cwh = np.ascontiguousarray(np.asarray(inputs["conv_w"], f32)[0].reshape(3, NFF, 128).transpose(2, 1, 0))
    cbh = _fm(np.asarray(inputs["conv_b"], f32)[0], NFF)
    xmT = np.ascontiguousarray(meta.T.reshape(8, 128, 16).transpose(1, 0, 2))
    if with_sample:
        xsm = np.asarray(inputs["x_sample"], f32)
        cmk = np.asarray(inputs["cache_meta_k"], f32)[0]
        cmv = np.asarray(inputs["cache_meta_v"], f32)[0]
        cwk = np.asarray(inputs["cache_win_k"], f32)[0]
        cwv = np.asarray(inputs["cache_win_v"], f32)[0]
        shg = np.asarray(inputs["state_hgrn"], f32)[0]
        scv = np.asarray(inputs["state_conv"], f32)[0]
        NBS = xsm.shape[0] // ncores
        assert NBS == 16

    nc = build_program(NSTM, with_sample, dbg)
    in_maps = []
    for c in range(ncores):
        xc = Gz[c * NSTM * 512: c * NSTM * 512 + NTOK]
        masks, rmask, rmask8, rowmask = _host_consts(c)
        m = {
            "xT": np.ascontiguousarray(xc.T.reshape(8, 128, NTOK).transpose(1, 0, 2)),
            "xtok": np.ascontiguousarray(xc),
            "xmT": xmT, "W": W, "masks": masks, "rmask": rmask, "lbraw": lbraw, "normg": normg,
            "sinks": sinks, "lnp": lnp, "cw": cwh, "cb": cbh,
        }
        if with_sample:
            bs = slice(16 * c, 16 * c + 16)
            xs = xsm[bs].reshape(128, D)
            dup = lambda a: np.ascontiguousarray(np.concatenate([a, a], axis=0))
            m.update({
                "xsT": np.ascontiguousarray(xs.T.reshape(8, 128, 128).transpose(1, 0, 2)),
                "xstok": np.ascontiguousarray(xs),
                "ckT": dup(np.concatenate([cmk[bs].transpose(3, 0, 2, 1), cwk[bs].transpose(3, 0, 2, 1)], axis=3)),
                "cmv": np.ascontiguousarray(cmv[bs].transpose(1, 0, 2, 3).reshape(16, 16, 256)),
                "cwv": np.ascontiguousarray(cwv[bs].transpose(1, 0, 2, 3).reshape(128, 16, 256)),
                "cwk_raw": np.ascontiguousarray(cwk[bs].reshape(16, 128 * 256)),
                "cwv_raw": np.ascontiguousarray(cwv[bs].reshape(16, 128 * 256)),
                "s0": np.ascontiguousarray(shg[bs].transpose(2, 0, 1, 3)),
                "scT": np.ascontiguousarray(scv[bs].reshape(16, 2, NFF, 128).transpose(3, 2, 0, 1)),
                "rmask8": rmask8, "rowmask": rowmask,
            })
        in_maps.append(m)
    res = run_bass_kernel_spmd(nc, in_maps, core_ids=list(range(ncores)))
    R = res.results
    if dbg is not None:
        return R
    y_prompt = np.concatenate([R[c]["y"] for c in range(ncores)], axis=0)[None]
    mkv = R[0]["metakv"]
    meta_k = np.ascontiguousarray(mkv[:, 0:256]).reshape(1, 1, 16, 4, 64)
    meta_v = np.ascontiguousarray(mkv[:, 256:512]).reshape(1, 1, 16, 4, 64)
    lkv = R[ncores - 1]["lastkv"]
    win_k = np.ascontiguousarray(lkv[:, 0:256]).reshape(1, 1, 128, 4, 64)
    win_v = np.ascontiguousarray(lkv[:, 256:512]).reshape(1, 1, 128, 4, 64)
    hg = np.ascontiguousarray(R[ncores - 1]["Sfin"].transpose(1, 0, 2))[None, None]
    cv = R[ncores - 1]["convp"]
    conv_p = np.ascontiguousarray(cv.transpose(2, 1, 0).reshape(2, DFF))[None, None]
    if not with_sample:
        return (y_prompt, None, meta_k, meta_v, win_k, win_v, hg, conv_p)
    y_sample = np.concatenate([R[c]["ys"].reshape(16, 8, D) for c in range(ncores)], axis=0)
    wks = np.concatenate([R[c]["wks"].reshape(16, 128, 4, 64) for c in range(ncores)], axis=0)[None]
    wvs = np.concatenate([R[c]["wvs"].reshape(16, 128, 4, 64) for c in range(ncores)], axis=0)[None]
    hs = np.concatenate([R[c]["hs"] for c in range(ncores)], axis=0)[None]
    cs = np.concatenate([np.ascontiguousarray(R[c]["convs"].transpose(2, 3, 1, 0)).reshape(16, 2, DFF) for c in range(ncores)], axis=0)[None]
    f = lambda a: np.ascontiguousarray(a, dtype=np.float32)
    return tuple(f(a) for a in (y_prompt, y_sample, meta_k, meta_v, win_k, win_v, hg, conv_p, wks, wvs, hs, cs))
```
